# Optimizing a Trainium2 kernel written in Bass

```python
import math
import jax, jax.numpy as jnp
from jax import lax
import numpy as np

D_MODEL = 1024
BATCH = 32
SEQ = 2048
DEPTH = 2
DEC_BATCH = 16
DEC_SEQ = 32
PAST_LEN = 2048

CHUNK = 64
Q_BLOCK = 128
ROPE_THETA = 10000.0
EPS = 1e-6
DA_HEAD_DIM = 64
DA_V_DIM = 2 * DA_HEAD_DIM
DA_HEADS = D_MODEL // DA_V_DIM
V_HEAD_DIM = 128
MLA_HEADS = D_MODEL // V_HEAD_DIM
NOPE_DIM = 128
ROPE_DIM = 64
Q_LORA = 384
KV_LORA = 256
MLA_SCALE = (NOPE_DIM + ROPE_DIM) ** -0.5
D_FF = 2816
CONV_W = 3
DA_Q_COLS = DA_HEADS * 2 * DA_HEAD_DIM
DA_K_COLS = DA_HEADS * 2 * DA_HEAD_DIM
DA_V_COLS = DA_HEADS * DA_V_DIM
GATE_COLS = 2 * D_MODEL
IN_COLS = DA_Q_COLS + DA_K_COLS + DA_V_COLS + Q_LORA + KV_LORA + ROPE_DIM + GATE_COLS

kernel_name = "diffattn_mla_convffn_streaming_step"


def rmsnorm(x, g):
    xf = x.astype(jnp.float32)
    y = xf * lax.rsqrt(jnp.mean(xf * xf, axis=-1, keepdims=True) + EPS)
    return (y * g.astype(jnp.float32)).astype(x.dtype)


def rope(x, pos):
    half = x.shape[-1] // 2
    inv = jnp.power(ROPE_THETA, -jnp.arange(half, dtype=jnp.float32) / half)
    ang = pos.astype(jnp.float32)[:, None] * inv[None, :]
    ang = ang.reshape((ang.shape[0],) + (1,) * (x.ndim - 3) + (half,))
    cos, sin = jnp.cos(ang), jnp.sin(ang)
    xf = x.astype(jnp.float32)
    x1, x2 = xf[..., :half], xf[..., half:]
    return jnp.concatenate([x1 * cos - x2 * sin, x2 * cos + x1 * sin], axis=-1).astype(x.dtype)


def chunk_mask(q_pos, k_pos):
    return (k_pos[None, :] // CHUNK) <= (q_pos[:, None] // CHUNK)


def masked_softmax(s, mask):
    return jax.nn.softmax(jnp.where(mask, s, -1e30), axis=-1)


def query_blocks(fn, qs, q_pos):
    S = q_pos.shape[0]
    if S <= Q_BLOCK or S % Q_BLOCK:
        return fn(qs, q_pos)
    nb = S // Q_BLOCK
    B = qs[0].shape[0]
    qs_b = tuple(jnp.moveaxis(q.reshape((B, nb, Q_BLOCK) + q.shape[2:]), 1, 0) for q in qs)
    out = lax.map(lambda a: fn(a[0], a[1]), (qs_b, q_pos.reshape(nb, Q_BLOCK)))
    out = jnp.moveaxis(out, 0, 1)
    return out.reshape((B, S) + out.shape[3:])


def diff_attention(q, q_pos, k, v, k_pos, lam):
    d = DA_HEAD_DIM
    scale = d ** -0.5
    mask = chunk_mask(q_pos, k_pos)
    s1 = jnp.einsum('bshd,bthd->bhst', q[..., :d], k[..., :d]).astype(jnp.float32) * scale
    s2 = jnp.einsum('bshd,bthd->bhst', q[..., d:], k[..., d:]).astype(jnp.float32) * scale
    a = (masked_softmax(s1, mask) - lam * masked_softmax(s2, mask)).astype(v.dtype)
    return jnp.einsum('bhst,bthv->bshv', a, v)


def latent_attention(q_lat, q_rope, q_pos, c_kv, k_rope, k_pos):
    mask = chunk_mask(q_pos, k_pos)
    s = (jnp.einsum('bshc,btc->bhst', q_lat, c_kv).astype(jnp.float32)
         + jnp.einsum('bshr,btr->bhst', q_rope, k_rope).astype(jnp.float32)) * MLA_SCALE
    p = masked_softmax(s, mask).astype(c_kv.dtype)
    return jnp.einsum('bhst,btc->bshc', p, c_kv)


def trunk_layer(x, past, lidx, w):
    (g_attn, w_in, b_gate, lam_q1, lam_k1, lam_q2, lam_k2, g_da_head, g_cq, w_uq, g_ckv,
     w_uk, w_uv, w_o, g_ffn, w_up, w_conv, b_conv, w_down) = w
    B, S, _ = x.shape
    past_len = 0 if past is None else past[0].shape[1]
    q_pos = past_len + jnp.arange(S, dtype=jnp.int32)
    k_pos = jnp.arange(past_len + S, dtype=jnp.int32)

    h = rmsnorm(x, g_attn)
    z = h @ w_in
    o1 = DA_Q_COLS
    o2 = o1 + DA_K_COLS
    o3 = o2 + DA_V_COLS
    o4 = o3 + Q_LORA
    o5 = o4 + KV_LORA
    o6 = o5 + ROPE_DIM
    q_da = rope(z[..., :o1].reshape(B, S, DA_HEADS, 2, DA_HEAD_DIM), q_pos).reshape(B, S, DA_HEADS, 2 * DA_HEAD_DIM)
    k_da = rope(z[..., o1:o2].reshape(B, S, DA_HEADS, 2, DA_HEAD_DIM), q_pos).reshape(B, S, DA_HEADS, 2 * DA_HEAD_DIM)
    v_da = z[..., o2:o3].reshape(B, S, DA_HEADS, DA_V_DIM)
    c_q = rmsnorm(z[..., o3:o4], g_cq)
    c_kv = rmsnorm(z[..., o4:o5], g_ckv)
    k_rope = rope(z[..., o5:o6], q_pos)
    gates = jax.nn.sigmoid(z[..., o6:] + b_gate)

    if past is None:
        k_all, v_all, ckv_all, kr_all = k_da, v_da, c_kv, k_rope
        prev_conv = jnp.zeros((B, CONV_W - 1, 2 * D_FF), x.dtype)
    else:
        k_all = jnp.concatenate([past[0], k_da], axis=1)
        v_all = jnp.concatenate([past[1], v_da], axis=1)
        ckv_all = jnp.concatenate([past[2], c_kv], axis=1)
        kr_all = jnp.concatenate([past[3], k_rope], axis=1)
        prev_conv = past[4]

    f32 = jnp.float32
    lam_init = 0.8 - 0.6 * math.exp(-0.3 * lidx)
    lam = (jnp.exp(jnp.sum(lam_q1.astype(f32) * lam_k1.astype(f32)))
           - jnp.exp(jnp.sum(lam_q2.astype(f32) * lam_k2.astype(f32))) + lam_init)
    o_da = query_blocks(lambda qs, qp: diff_attention(qs[0], qp, k_all, v_all, k_pos, lam), (q_da,), q_pos)
    o_da = (rmsnorm(o_da, g_da_head) * (1.0 - lam_init)).reshape(B, S, D_MODEL)

    q = jnp.einsum('bsc,chn->bshn', c_q, w_uq)
    q_rope = rope(q[..., NOPE_DIM:], q_pos)
    q_lat = jnp.einsum('bshn,hcn->bshc', q[..., :NOPE_DIM], w_uk)
    o_lat = query_blocks(lambda qs, qp: latent_attention(qs[0], qs[1], qp, ckv_all, kr_all, k_pos), (q_lat, q_rope), q_pos)
    o_mla = jnp.einsum('bshc,hcv->bshv', o_lat, w_uv).reshape(B, S, D_MODEL)

    x = x + (gates[..., :D_MODEL] * o_da + gates[..., D_MODEL:] * o_mla) @ w_o

    u = rmsnorm(x, g_ffn) @ w_up
    padded = jnp.concatenate([prev_conv, u], axis=1)
    c = b_conv + w_conv[0] * padded[:, 0:S]
    for j in range(1, CONV_W):
        c = c + w_conv[j] * padded[:, j:j + S]
    x = x + (jax.nn.silu(c[..., :D_FF]) * c[..., D_FF:]) @ w_down
    return x, (k_da, v_da, c_kv, k_rope, padded[:, S:])


def setup_inputs(seed: int = 0) -> dict:
    key = jax.random.key(seed)
    ks = iter(jax.random.split(key, 40))
    f32 = jnp.float32

    def nrm(shape, scale):
        return jax.random.normal(next(ks), shape, f32) * scale

    def gain(shape):
        return 1.0 + 0.01 * jax.random.normal(next(ks), shape, f32)

    L = DEPTH
    return {
        "x_prompt": nrm((BATCH, SEQ, D_MODEL), 1.0),
        "x_sample": nrm((DEC_BATCH, DEC_SEQ, D_MODEL), 1.0),
        "cache_dk": nrm((L, DEC_BATCH, PAST_LEN, DA_HEADS, 2 * DA_HEAD_DIM), 1.0),
        "cache_dv": nrm((L, DEC_BATCH, PAST_LEN, DA_HEADS, DA_V_DIM), 1.0),
        "cache_ckv": nrm((L, DEC_BATCH, PAST_LEN, KV_LORA), 1.0),
        "cache_krope": nrm((L, DEC_BATCH, PAST_LEN, ROPE_DIM), 1.0),
        "state_conv": nrm((L, DEC_BATCH, CONV_W - 1, 2 * D_FF), 1.0),
        "g_attn": gain((L, D_MODEL)),
        "w_in": nrm((L, D_MODEL, IN_COLS), D_MODEL ** -0.5),
        "b_gate": nrm((L, GATE_COLS), 0.01),
        "lam_q1": nrm((L, DA_HEAD_DIM), 0.1),
        "lam_k1": nrm((L, DA_HEAD_DIM), 0.1),
        "lam_q2": nrm((L, DA_HEAD_DIM), 0.1),
        "lam_k2": nrm((L, DA_HEAD_DIM), 0.1),
        "g_da_head": gain((L, DA_V_DIM)),
        "g_cq": gain((L, Q_LORA)),
        "w_uq": nrm((L, Q_LORA, MLA_HEADS, NOPE_DIM + ROPE_DIM), Q_LORA ** -0.5),
        "g_ckv": gain((L, KV_LORA)),
        "w_uk": nrm((L, MLA_HEADS, KV_LORA, NOPE_DIM), KV_LORA ** -0.5),
        "w_uv": nrm((L, MLA_HEADS, KV_LORA, V_HEAD_DIM), KV_LORA ** -0.5),
        "w_o": nrm((L, D_MODEL, D_MODEL), D_MODEL ** -0.5),
        "g_ffn": gain((L, D_MODEL)),
        "w_up": nrm((L, D_MODEL, 2 * D_FF), D_MODEL ** -0.5),
        "w_conv": nrm((L, CONV_W, 2 * D_FF), CONV_W ** -0.5),
        "b_conv": nrm((L, 2 * D_FF), 0.01),
        "w_down": nrm((L, D_FF, D_MODEL), D_FF ** -0.5),
        "g_final": gain((D_MODEL,)),
    }


def reference(x_prompt, x_sample, cache_dk, cache_dv, cache_ckv, cache_krope, state_conv,
              g_attn, w_in, b_gate, lam_q1, lam_k1, lam_q2, lam_k2, g_da_head, g_cq, w_uq,
              g_ckv, w_uk, w_uv, w_o, g_ffn, w_up, w_conv, b_conv, w_down, g_final):
    weights = (g_attn, w_in, b_gate, lam_q1, lam_k1, lam_q2, lam_k2, g_da_head, g_cq, w_uq,
               g_ckv, w_uk, w_uv, w_o, g_ffn, w_up, w_conv, b_conv, w_down)
    yp, ys = x_prompt, x_sample
    st_p, st_s = [], []
    for l in range(DEPTH):
        w = tuple(a[l] for a in weights)
        yp, sp = trunk_layer(yp, None, l, w)
        past = (cache_dk[l], cache_dv[l], cache_ckv[l], cache_krope[l], state_conv[l])
        ys, ss = trunk_layer(ys, past, l, w)
        st_p.append(sp)
        st_s.append(ss)
    y_prompt = rmsnorm(yp, g_final)
    y_sample = rmsnorm(ys, g_final)
    new_dk_p = jnp.stack([s[0] for s in st_p])
    new_dv_p = jnp.stack([s[1] for s in st_p])
    new_ckv_p = jnp.stack([s[2] for s in st_p])
    new_kr_p = jnp.stack([s[3] for s in st_p])
    new_conv_p = jnp.stack([s[4] for s in st_p])
    new_dk_s = jnp.stack([s[0] for s in st_s])
    new_dv_s = jnp.stack([s[1] for s in st_s])
    new_ckv_s = jnp.stack([s[2] for s in st_s])
    new_kr_s = jnp.stack([s[3] for s in st_s])
    new_conv_s = jnp.stack([s[4] for s in st_s])
    return (y_prompt, y_sample, new_dk_p, new_dv_p, new_ckv_p, new_kr_p, new_conv_p,
            new_dk_s, new_dv_s, new_ckv_s, new_kr_s, new_conv_s)
```

```python
import math
from contextlib import ExitStack

import numpy as np
import concourse.bass as bass
import concourse.mybir as mybir
from concourse.bass_utils import run_bass_kernel_spmd

F32 = mybir.dt.float32
BF16 = mybir.dt.bfloat16
AF = mybir.ActivationFunctionType
ALU = mybir.AluOpType
AX = mybir.AxisListType

D = 1024
NH = 8
QL = 384
KVL = 256
RD = 64
DFF = 2816
NCH = 44
EPS = 1e-6
CHUNK = 64
MLA_SCALE = (128 + 64) ** -0.5
NT = 35
SPW = 708
SP_BG, SP_GCQ, SP_GDA, SP_WC, SP_BC, SP_LAM, SP_GCKV = 0, 16, 19, 20, 152, 196, 452
T_IN, T_UQN, T_UQR, T_UKT, T_UV, T_G, T_O, T_UP, T_DN = 0, 8, 9, 10, 11, 12, 16, 18, 29
IN_NCOLS = [512] * 6 + [384, 320]


class Buf:
    __slots__ = ("name", "w", "r", "excl")

    def __init__(self, name, excl=False):
        self.name = name
        self.w = None
        self.r = []
        self.excl = excl


class Op:
    __slots__ = ("eng", "fn", "deps", "dma", "sig", "val", "tag", "users")

    def __init__(self, eng, fn, dma, tag):
        self.eng = eng
        self.fn = fn
        self.deps = []
        self.dma = dma
        self.sig = False
        self.val = 0
        self.tag = tag
        self.users = 0


class Rec:
    ENGS = ("pe", "act", "dve", "pool", "sp")

    def __init__(self):
        self.ops = {e: [] for e in self.ENGS}
        self.tag_ops = {}
        self.wait_all_tags = set()
        self.final_deps = []

    def add(self, eng, fn, reads=(), writes=(), dma=False, tag=None):
        o = Op(eng, fn, dma, tag)
        deps = {}
        for b in reads:
            if b.w is not None:
                deps[id(b.w)] = b.w
            if b.excl:
                for r in b.r:
                    if r.eng != eng:
                        deps[id(r)] = r
        for b in writes:
            if b.w is not None:
                deps[id(b.w)] = b.w
            for r in b.r:
                deps[id(r)] = r
        for b in reads:
            if not dma:
                b.r = [r for r in b.r if r.dma or r.eng != eng]
            b.r.append(o)
        for b in writes:
            b.w = o
            b.r = []
        for d in deps.values():
            if d is o:
                continue
            if (not d.dma) and (not dma) and d.eng == "pe" and eng == "pe":
                continue
            o.deps.append(d)
            d.users += 1
        if dma:
            lst = self.tag_ops.setdefault(tag, [])
            lst.append(o)
            o.val = 16 * len(lst)
        self.ops[eng].append(o)
        return o


class Cfg:
    def __init__(self, L=2, S=2048, NPS=4, NSS=2, SD=32, PAST=2048):
        self.L, self.S, self.NPS, self.NSS, self.SD, self.PAST = L, S, NPS, NSS, SD, PAST


def build(cfg):
    L, S, NPS, NSS, SD, PAST = cfg.L, cfg.S, cfg.NPS, cfg.NSS, cfg.SD, cfg.PAST
    NTOK = max(S, PAST + SD)
    NKT = (NTOK + 127) // 128
    nc = bass.Bass("TRN2", target_bir_lowering=False)
    R = Rec()

    def din(name, shape, dt=F32):
        return nc.dram_tensor(name, list(shape), dt, kind="ExternalInput").ap()

    def dout(name, shape):
        return nc.dram_tensor(name, list(shape), F32, kind="ExternalOutput").ap()

    xp = din("xp", [NPS, S, D])
    xs = din("xs", [NSS, SD, D])
    cdk = din("cdk", [L, NSS, PAST, 1024])
    cdv = din("cdv", [L, NSS, PAST, 1024])
    cckv = din("cckv", [L, NSS, PAST, KVL])
    ckr = din("ckr", [L, NSS, PAST, RD])
    sconv = din("sconv", [L, NSS, 2, 2 * DFF])
    wpack = din("wpack", [L, NT, 128, 4096])
    spd = din("spd", [128, L * SPW])
    bcd = din("bcd", [2 * L + 1, 128, D])
    rtp = din("rtp", [S, 2, 64])
    rts = din("rts", [SD, 2, 64])
    rfp = din("rfp", [2, 128, S])
    rfs = din("rfs", [2, 128, SD])
    identd = din("identd", [128, 128])

    o_yp = dout("o_yp", [NPS, S, D])
    o_ys = dout("o_ys", [NSS, SD, D])
    o_dk = [dout("o_dkp", [L, NPS, S, 1024]), dout("o_dks", [L, NSS, SD, 1024])]
    o_dv = [dout("o_dvp", [L, NPS, S, 1024]), dout("o_dvs", [L, NSS, SD, 1024])]
    o_ckv = [dout("o_ckvp", [L, NPS, S, KVL]), dout("o_ckvs", [L, NSS, SD, KVL])]
    o_kr = [dout("o_krp", [L, NPS, S, RD]), dout("o_krs", [L, NSS, SD, RD])]
    o_cv = [dout("o_cvp", [L, NPS, 2, 2 * DFF]), dout("o_cvs", [L, NSS, 2, 2 * DFF])]

    wscr = nc.dram_tensor("wscr", [L, NT, 128, 4096], BF16, kind="Internal").ap()
    x1d = nc.dram_tensor("x1d", [max(S, SD), D], F32, kind="Internal").ap()

    es = ExitStack()

    def sb(name, cols, dt=F32):
        return es.enter_context(nc.sbuf_tensor(name, [128, cols], dt))

    kT = sb("kT", NH * NTOK, BF16)
    kTv = kT[:].rearrange("p (h t) -> p h t", h=NH)
    vda = sb("vda", NKT * 1024, BF16)
    vdav = vda[:].rearrange("p (k c) -> p k c", k=NKT)
    ckvT = sb("ckvT", 2 * NTOK, BF16)
    ckvTv = ckvT[:].rearrange("p (c t) -> p c t", c=2)
    ckvt = sb("ckvt", NKT * KVL, BF16)
    ckvtv = ckvt[:].rearrange("p (k c) -> p k c", k=NKT)
    krT = sb("krT", NTOK, BF16)
    xb = sb("xb", 4 * D, F32)
    xbv = xb[:].rearrange("p (t d) -> p t d", t=4)
    hT = sb("hT", 8 * 512, BF16)
    hTv = hT[:].rearrange("p (k n) -> p k n", k=8)
    hns = [sb("hn0", D, BF16), sb("hn1", D, BF16)]
    gbc = sb("gbc", D, F32)
    wsl = [sb("wsl%d" % i, 4096, BF16) for i in range(2)]
    qdaT = sb("qdaT", 8 * 512, BF16)
    qdaTv = qdaT[:].rearrange("p (h n) -> p h n", h=8)
    qpad = [sb("qpad%d" % i, 2 * 512, BF16) for i in range(2)]
    r1 = sb("r1", 24576 // 2, BF16)
    odaT = r1[:, 0:4096].rearrange("p (h n) -> p h n", h=8)
    omlaT = r1[:, 4096:8192].rearrange("p (h n) -> p h n", h=8)
    cqT = r1[:, 8192:8192 + 1536].rearrange("p (k n) -> p k n", k=3)
    qn = r1[:, 9728:9728 + 512]
    qlat = r1[:, 10240:10240 + 1024].rearrange("p (c n) -> p c n", c=2)
    olat = r1[:, 11264:11264 + 1024].rearrange("p (c n) -> p c n", c=2)
    mT = r1[:, 8192:12288].rearrange("p (h n) -> p h n", h=8)
    aT = r1[:, 0:22 * 512].rearrange("p (c n) -> p c n", c=22)
    qrT = sb("qrT", 8 * 512, BF16)
    qrTv = qrT[:].rearrange("p (h n) -> p h n", h=8)
    r2 = sb("r2", 20480 // 4, F32)
    stg1s = [r2[:, 0:512], r2[:, 512:1024]]
    stg2s = [r2[:, 1024:1536], r2[:, 1536:2048]]
    stgos = [r2[:, 2048:2560], r2[:, 2560:3072]]
    rtok = r2[:, 3072:3584].rearrange("p (t a d) -> p t a d", t=4, a=2)
    r2bB = r2[:, 3584:4352].bitcast(BF16)
    qtoks = [r2bB[:, 0:512], r2bB[:, 512:1024]]
    cqn = r2bB[:, 1024:1408]
    krdup = r2bB[:, 1408:1536]
    stg1, stg2, stgo, qtok = stg1s[0], stg2s[0], stgos[0], qtoks[0]
    nr1 = r2[:, 0:512]
    nr2 = r2[:, 512:1024]
    rfm = r2[:, 1024:2048].rearrange("p (a n) -> p a n", a=2)
    r2bC = r2[:, 2048:5120].bitcast(BF16)
    NPS_ = 6
    Psl = [r2bC[:, i * 512:(i + 1) * 512] for i in range(NPS_)]
    sqb = r2bC[:, 3072:3584]
    qn_b = r2bC[:, 3584:4096]
    qlat_b = r2bC[:, 4096:5120].rearrange("p (c n) -> p c n", c=2)
    olat_b = r2bC[:, 5120:6144].rearrange("p (c n) -> p c n", c=2)
    ues = [[r2[:, (2 * sd + k) * 514:(2 * sd + k + 1) * 514] for k in range(2)] for sd in range(2)]
    ccs = [[r2[:, 2056 + (2 * sd + k) * 512:2056 + (2 * sd + k + 1) * 512] for k in range(2)] for sd in range(2)]
    spt = sb("spt", L * SPW, F32)
    identf = sb("identf", 128, F32)
    identb = sb("identb", 128, BF16)
    onesb = sb("onesb", 128, BF16)
    onesm = sb("onesm", 128, BF16)
    carry = sb("carry", 88, F32)
    carv = carry[:].rearrange("p (t c) -> p t c", t=2)
    stgc = sb("stgc", 128, F32)
    stat = sb("stat", 64, F32)
    lamt = sb("lamt", 8 * L + 8, F32)

    ps = [es.enter_context(nc.psum_tensor("ps%d" % i, [128, 512], F32)) for i in range(8)]
    psb = [Buf("ps%d" % i, excl=True) for i in range(8)]

    def psf(i):
        return ps[i][:]

    def psh(i):
        return ps[i][:].bitcast(BF16)

    B = {}

    def bf(name):
        if name not in B:
            B[name] = Buf(name)
        return B[name]

    wslb = [bf("wsl0"), bf("wsl1")]
    wctr = [0]

    def pe(fn, r, w):
        return R.add("pe", fn, r, w)

    unread = {}

    def _op(eng, meth, r, w, kw):
        for rb in r:
            if rb.excl:
                unread[rb.name] = None
        return R.add(eng, lambda e: getattr(e, meth)(**kw), r, w)

    def act(meth, r, w, **kw):
        return _op("act", meth, r, w, kw)

    def dve(meth, r, w, **kw):
        return _op("dve", meth, r, w, kw)

    def pool(meth, r, w, **kw):
        return _op("pool", meth, r, w, kw)

    def fence(new_names, old_names):
        ops = []
        for n in old_names:
            b = bf(n)
            if b.w is not None:
                ops.append(b.w)
            ops += b.r
        for n in new_names:
            nb = bf(n)
            nb.r = nb.r + ops

    def load(out_ap, in_ap, r, w, tag):
        return R.add("sp", lambda e: e.dma_start(out=out_ap, in_=in_ap), r, w, dma=True, tag=tag)

    def store(out_ap, in_ap, r, tag, w=(), q="pool"):
        o = R.add(q, lambda e: e.dma_start(out=out_ap, in_=in_ap), r, w, dma=True, tag=tag)
        R.final_deps.append(o)
        return o

    def mm(out, lhsT, rhs, start, stop, r, w):
        if start and unread.get(w[0].name) == "mm":
            raise AssertionError("PSUM bank %s overwritten before being read" % w[0].name)
        if stop:
            unread[w[0].name] = "mm"
        return pe(lambda e: e.matmul(out, lhsT=lhsT, rhs=rhs, start=start, stop=stop), r, w)

    def tp(out, in_, ident, r, w):
        unread[w[0].name] = "tp"
        return pe(lambda e: e.transpose(out, in_, ident), r, w)

    def wload(l, t):
        i = wctr[0] % 2
        wctr[0] += 1
        load(wsl[i][:], wscr[l, t], [bf("wscr%d_%d" % (l, t))], [wslb[i]], "wsl%d" % i)
        return wsl[i][:], wslb[i]

    load(identf[:], identd[:, :], [], [bf("identf")], "c_ident")
    load(spt[:], spd[:, :], [], [bf("spt")], "c_spt")
    dve("tensor_copy", [bf("identf")], [bf("identb")], out=identb[:], in_=identf[:])
    dve("memset", [], [bf("onesb")], ap=onesb[:], constant=1.0)
    dve("memset", [], [bf("onesm")], ap=onesm[:], constant=1.0 / 128.0)
    dve("memset", [], [bf("epsc")], ap=stat[:, 0:1], constant=EPS)
    pool("memset", [], [bf("qpad0")], ap=qpad[0][:], constant=0.0)
    pool("memset", [], [bf("qpad1")], ap=qpad[1][:], constant=0.0)
    pool("memset", [], [bf("qrT")], ap=qrT[:], constant=0.0)
    epsc = stat[:, 0:1]
    for l in range(L):
        for t in range(NT):
            if t == T_UV:
                continue
            R.add("pool", (lambda o_, i_: (lambda e: e.dma_start(out=o_, in_=i_)))(wscr[l, t], wpack[l, t]),
                  [], [bf("wscr%d_%d" % (l, t))], dma=True, tag="cast%d" % l)
        R.wait_all_tags.add("cast%d" % l)
    lam_init = [0.8 - 0.6 * math.exp(-0.3 * l) for l in range(L)]
    LB = bf("lamt")
    for l in range(L):
        base = l * SPW + SP_LAM
        for j in range(2):
            dve("tensor_tensor", [bf("spt")], [bf("stg1_0")], out=stg1[:, 0:64], in0=spt[:, base + 128 * j: base + 128 * j + 64],
                in1=spt[:, base + 128 * j + 64: base + 128 * j + 128], op=ALU.mult)
            dve("tensor_reduce", [bf("stg1_0")], [LB], out=lamt[:, 4 * l + j:4 * l + j + 1], in_=stg1[:, 0:64], axis=AX.X, op=ALU.add)
        act("activation", [LB], [LB], out=lamt[:, 4 * l:4 * l + 2], in_=lamt[:, 4 * l:4 * l + 2], func=AF.Exp)
        dve("tensor_tensor", [LB], [LB], out=lamt[:, 4 * l + 2:4 * l + 3], in0=lamt[:, 4 * l + 1:4 * l + 2],
            in1=lamt[:, 4 * l:4 * l + 1], op=ALU.subtract)
        dve("tensor_scalar", [LB], [LB], out=lamt[:, 4 * l + 3:4 * l + 4], in0=lamt[:, 4 * l + 2:4 * l + 3],
            scalar1=-lam_init[l], scalar2=None, op0=ALU.add)
        dve("memset", [], [LB], ap=lamt[:, 4 * L + l:4 * L + l + 1], constant=math.log(1.0 - lam_init[l]))

    import os as _os
    _stop = _os.environ.get("KSTOP", "")
    _ck = [0]

    class _Stop(Exception):
        pass

    def ckpt(name):
        _ck[0] += 1
        if _stop and _stop == "%s@%d" % (name, _ck[0]):
            raise _Stop()
        if _os.environ.get("KCKPT"):
            print("ckpt %s@%d" % (name, _ck[0]), {e: len(R.ops[e]) for e in R.ENGS})

    stat_ctr = [0]

    def rms_stats(in_ap, Pt, ncols, rbufs, junk_ap, junk_buf):
        stat_ctr[0] += 1
        k = stat_ctr[0] % 20
        c = 1 + k * 3
        ss, lnv, rs, sbuf_ = stat[:, c:c + 1], stat[:, c + 1:c + 2], stat[:, c + 2:c + 3], bf("stat%d" % k)
        act("activation", rbufs, [junk_buf, sbuf_], out=junk_ap, in_=in_ap, func=AF.Square, accum_out=ss[:Pt, :])
        act("activation", [sbuf_, bf("epsc")], [sbuf_], out=lnv[:Pt, :], in_=ss[:Pt, :], func=AF.Ln, scale=1.0 / ncols, bias=epsc[:Pt, :])
        act("activation", [sbuf_], [sbuf_], out=rs[:Pt, :], in_=lnv[:Pt, :], func=AF.Exp, scale=-0.5)
        return rs, sbuf_

    B_TMP = ["stg1_0", "stg1_1", "stg2_0", "stg2_1", "stgo_0", "stgo_1", "rtok", "qtok_0", "qtok_1", "cqn", "krdup"]
    CD_TMP = ["nr1", "nr2", "rfm", "sqb", "qn1", "qlat1", "olat0_1", "olat1_1"] + ["P%d" % i for i in range(NPS_)]
    FFN_TMP = ["ue%d_%d" % (sd, k) for sd in range(2) for k in range(2)] + ["cc%d_%d" % (sd, k) for sd in range(2) for k in range(2)]
    R1_ATT = ["odaT%d" % h for h in range(8)] + ["omlaT%d" % h for h in range(8)] + ["cqT", "qn0", "qlat0", "olat0_0", "olat1_0"]
    R1_MT = ["mT%d" % h for h in range(8)]
    R1_AT = ["aT%d" % c for c in range(22)]
    pctr = [0]
    rot = [0, 0, 0]

    def run_sequence(grp, si, Sq, Pt, past):
        NB = (Sq + 511) // 512
        kt_base0 = past // 128
        x_in = xp[si] if grp == 0 else xs[si]
        y_out = o_yp[si] if grp == 0 else o_ys[si]
        rt_tok = rtp if grp == 0 else rts
        rt_fm = rfp if grp == 0 else rfs
        for l in range(L):
            spb = l * SPW
            fence(B_TMP, FFN_TMP + CD_TMP)
            fence(R1_ATT, R1_AT + R1_MT)
            if past == 0:
                pool("memset", [], [bf("carry")], ap=carry[:], constant=0.0)
            else:
                load(stgc[:88, :], sconv[l, si].rearrange("t (c p) -> (t c) p", p=128), [], [bf("stgc")], "stgc_in")
                tp(psf(7)[:, 0:88], stgc[:88, :], identf[:88, :88], [bf("stgc"), bf("identf")], [psb[7]])
                dve("tensor_copy", [psb[7]], [bf("carry")], out=carry[:], in_=psf(7)[:, 0:88])
                for kt in range(past // 128):
                    tsl = slice(kt * 128, (kt + 1) * 128)
                    for half in range(2):
                        load(gbc[:, 0:512], cdk[l, si, tsl, half * 512:(half + 1) * 512], [], [bf("gbc")], "gbc")
                        dve("tensor_copy", [bf("gbc")], [bf("qtok_0")], out=qtok, in_=gbc[:, 0:512])
                        for hh in range(4):
                            tp(psh(6)[:, hh * 128:(hh + 1) * 128], qtok[:, hh * 128:(hh + 1) * 128], identb[:],
                               [bf("qtok_0"), bf("identb")], [psb[6]])
                        act("activation", [psb[6]], [bf("kT")], out=kTv[:, 4 * half:4 * half + 4, tsl],
                            in_=psh(6)[:, 0:512].rearrange("p (h t) -> p h t", h=4), func=AF.Copy)
                        load(gbc[:, 512:1024], cdv[l, si, tsl, half * 512:(half + 1) * 512], [], [bf("gbc2")], "gbc2")
                        pool("tensor_copy", [bf("gbc2")], [bf("vda")], out=vdav[:, kt, half * 512:(half + 1) * 512], in_=gbc[:, 512:1024])
                    load(stg1[:, 0:256], cckv[l, si, tsl, :], [], [bf("stg1_0")], "stg1_in")
                    load(stg2[:, 0:64], ckr[l, si, tsl, :], [], [bf("stg2_0")], "stg2_in")
                    dve("tensor_copy", [bf("stg1_0")], [bf("ckvt")], out=ckvtv[:, kt, :], in_=stg1[:, 0:256])
                    dve("tensor_copy", [bf("stg2_0")], [bf("krdup")], out=krdup.rearrange("p (a d) -> p a d", a=2),
                        in_=stg2[:, 0:64].unsqueeze(1).broadcast_to([128, 2, 64]))
                    for c in range(2):
                        tp(psh(7)[:, c * 128:(c + 1) * 128], ckvtv[:, kt, c * 128:(c + 1) * 128], identb[:],
                           [bf("ckvt"), bf("identb")], [psb[7]])
                    tp(psh(7)[:, 256:384], krdup, identb[:], [bf("krdup"), bf("identb")], [psb[7]])
                    act("activation", [psb[7]], [bf("ckvT")], out=ckvTv[:, :, tsl],
                        in_=psh(7)[:, 0:256].rearrange("p (c t) -> p c t", c=2), func=AF.Copy)
                    act("activation", [psb[7]], [bf("krT")], out=krT[:, tsl], in_=psh(7)[:, 256:384], func=AF.Copy)

            for j in range(NB):
                if j > 0:
                    fence(B_TMP, FFN_TMP + CD_TMP)
                    fence(R1_ATT, R1_AT + R1_MT)
                N = min(512, Sq - j * 512)
                nTT = N // Pt
                kt_base = kt_base0 + j * 4
                t0 = j * 512
                xbb = [bf("xb%d" % tt) for tt in range(nTT)]
                hTb = [bf("hT%d" % tt) for tt in range(nTT)]
                ktl = [(kt, 128, 0, False) for kt in range(kt_base)]
                if grp == 0:
                    ktl += [(kt_base + i, 128, i * 128, True) for i in range(nTT)]
                else:
                    ktl += [(kt_base, Pt, 0, False)]
                nlast = len(ktl) - 1

                for tt in range(nTT):
                    rows = slice(t0 + tt * Pt, t0 + (tt + 1) * Pt)
                    src = x_in[rows, :] if l == 0 else x1d[rows, :]
                    rd = [] if l == 0 else [bf("x1d%d_%d" % (j, tt))]
                    load(xbv[:Pt, tt, :], src, rd, [xbb[tt]], "xb%d" % tt)
                load(rtok[:Pt, 0:nTT], rt_tok[t0:t0 + N].rearrange("(t p) a d -> p t a d", p=Pt), [], [bf("rtok")], "rtok")

                def norm_hT(gidx):
                    load(gbc[:], bcd[gidx], [], [bf("gbc"), bf("gbc2")], "gbc")
                    sts = []
                    for tt in range(nTT):
                        k = tt % 2
                        sts.append(rms_stats(xbv[:Pt, tt, :], Pt, D, [xbb[tt]], hns[k][:Pt, :], bf("hn%d" % k)))
                    for tt in range(nTT):
                        k = tt % 2
                        rs, rsb = sts[tt]
                        hn_, hnb_ = hns[k], bf("hn%d" % k)
                        pb = 6 + k
                        dve("scalar_tensor_tensor", [xbb[tt], rsb, bf("gbc"), bf("gbc2")], [hnb_], out=hn_[:Pt, :],
                            in0=xbv[:Pt, tt, :], scalar=rs[:Pt, :], in1=gbc[:Pt, :], op0=ALU.mult, op1=ALU.mult)
                        for kc in range(8):
                            tp(psh(pb)[:, kc * 128:kc * 128 + Pt], hn_[:Pt, kc * 128:(kc + 1) * 128], identb[:Pt, :Pt],
                               [hnb_, bf("identb")], [psb[pb]])
                        dve("tensor_copy", [psb[pb]], [hTb[tt]], out=hTv[:, :, tt * Pt:(tt + 1) * Pt],
                            in_=psh(pb).rearrange("p (k t) -> p k t", k=8)[:, :, :Pt])

                ckpt("A")
                norm_hT(l)

                ckpt("B")
                def rope_tok(psin, ncol, tt, outap, outbuf, rbank):
                    k = rot[0] % 2
                    rot[0] += 1
                    s1, s2, b1, b2 = stg1s[k], stg2s[k], bf("stg1_%d" % k), bf("stg2_%d" % k)
                    G = ncol // 64
                    cf = rtok[:Pt, tt, 0, :].unsqueeze(1).broadcast_to([Pt, G, 64])
                    sg = rtok[:Pt, tt, 1, :].unsqueeze(1).broadcast_to([Pt, G, 64])
                    dve("tensor_tensor", [rbank, bf("rtok")], [b1], out=s1[:Pt, :ncol].rearrange("p (g d) -> p g d", d=64),
                        in0=psin.rearrange("p (g d) -> p g d", d=64), in1=cf, op=ALU.mult)
                    dve("tensor_tensor", [rbank, bf("rtok")], [b2], out=s2[:Pt, :ncol].rearrange("p (g d) -> p g d", d=64),
                        in0=psin.rearrange("p (g d) -> p g d", d=64), in1=sg, op=ALU.mult)
                    t1v = s1[:Pt, :ncol].rearrange("p (g a d) -> p g a d", a=2, d=32)
                    t2v = s2[:Pt, :ncol].rearrange("p (g a d) -> p g a d", a=2, d=32)
                    ov = outap.rearrange("p (g a d) -> p g a d", a=2, d=32)
                    pool("tensor_tensor", [b1, b2], [outbuf], out=ov[:, :, 0, :], in0=t1v[:, :, 0, :], in1=t2v[:, :, 1, :], op=ALU.add)
                    pool("tensor_tensor", [b1, b2], [outbuf], out=ov[:, :, 1, :], in0=t1v[:, :, 1, :], in1=t2v[:, :, 0, :], op=ALU.add)

                def postB(g, tt, bi, zp):
                    tks = slice(tt * Pt, (tt + 1) * Pt)
                    kpos = slice(kt_base * 128 + tt * Pt, kt_base * 128 + (tt + 1) * Pt)
                    orow = slice(t0 + tt * Pt, t0 + (tt + 1) * Pt)
                    kti = kt_base + tt
                    ko = rot[1] % 2
                    kq = rot[2] % 2
                    so, sob, sotag = stgos[ko], bf("stgo_%d" % ko), "stgo_%d" % ko
                    qt, qtb = qtoks[kq], bf("qtok_%d" % kq)
                    if g < 2:
                        rot[2] += 1
                        rope_tok(zp, 512, tt, qt[:Pt, :], qtb, psb[bi])
                        for hh in range(4):
                            tp(psh(6)[:, hh * 128:hh * 128 + Pt], qt[:Pt, hh * 128:(hh + 1) * 128], identb[:Pt, :Pt],
                               [qtb, bf("identb")], [psb[6]])
                        act("activation", [psb[6]], [bf("qdaT%d" % g)], out=qdaTv[:, 4 * g:4 * g + 4, tks],
                            in_=psh(6)[:, 0:512].rearrange("p (h t) -> p h t", h=4)[:, :, :Pt], func=AF.Copy)
                    elif g < 4:
                        rot[1] += 1
                        rot[2] += 1
                        gg = g - 2
                        rope_tok(zp, 512, tt, so[:Pt, :], sob, psb[bi])
                        store(o_dk[grp][l, si, orow, gg * 512:(gg + 1) * 512], so[:Pt, :], [sob], sotag, q="act")
                        act("activation", [sob], [qtb], out=qt[:Pt, :], in_=so[:Pt, :], func=AF.Copy)
                        for hh in range(4):
                            tp(psh(6)[:, hh * 128:hh * 128 + Pt], qt[:Pt, hh * 128:(hh + 1) * 128], identb[:Pt, :Pt],
                               [qtb, bf("identb")], [psb[6]])
                        act("activation", [psb[6]], [bf("kT")], out=kTv[:, 4 * gg:4 * gg + 4, kpos],
                            in_=psh(6)[:, 0:512].rearrange("p (h t) -> p h t", h=4)[:, :, :Pt], func=AF.Copy)
                    elif g < 6:
                        rot[1] += 1
                        gg = g - 4
                        act("activation", [psb[bi]], [sob], out=so[:Pt, :], in_=zp, func=AF.Copy)
                        store(o_dv[grp][l, si, orow, gg * 512:(gg + 1) * 512], so[:Pt, :], [sob], sotag, q="act")
                        dve("tensor_copy", [psb[bi]], [bf("vda")], out=vdav[:Pt, kti, gg * 512:(gg + 1) * 512], in_=zp)
                    elif g == 6:
                        rs, rsb = rms_stats(zp, Pt, QL, [psb[bi]], cqn[:Pt, :], bf("cqn"))
                        dve("tensor_scalar", [psb[bi], rsb], [bf("cqn")], out=cqn[:Pt, :], in0=zp, scalar1=rs[:Pt, :], scalar2=None, op0=ALU.mult)
                        for kc in range(3):
                            tp(psh(6)[:, kc * 128:kc * 128 + Pt], cqn[:Pt, kc * 128:(kc + 1) * 128], identb[:Pt, :Pt],
                               [bf("cqn"), bf("identb")], [psb[6]])
                        for kc in range(3):
                            act("activation", [psb[6], bf("spt")], [bf("cqT")], out=cqT[:, kc, tks], in_=psh(6)[:, kc * 128:kc * 128 + Pt],
                                func=AF.Identity, scale=spt[:, spb + SP_GCQ + kc:spb + SP_GCQ + kc + 1])
                    else:
                        rot[1] += 1
                        rs, rsb = rms_stats(zp[:, 0:KVL], Pt, KVL, [psb[bi]], cqn[:Pt, 0:KVL], bf("cqn"))
                        rope_tok(zp[:, KVL:KVL + 64], 64, tt, so[:Pt, KVL:KVL + 64], sob, psb[bi])
                        dve("scalar_tensor_tensor", [psb[bi], rsb, bf("spt")], [sob], out=so[:Pt, 0:KVL], in0=zp[:, 0:KVL],
                            scalar=rs[:Pt, :], in1=spt[:Pt, spb + SP_GCKV:spb + SP_GCKV + KVL], op0=ALU.mult, op1=ALU.mult)
                        store(o_ckv[grp][l, si, orow, :], so[:Pt, 0:KVL], [sob], sotag, q="act")
                        store(o_kr[grp][l, si, orow, :], so[:Pt, KVL:KVL + 64], [sob], sotag + "b", q="act")
                        dve("tensor_copy", [sob], [bf("ckvt")], out=ckvtv[:Pt, kti, :], in_=so[:Pt, 0:KVL])
                        dve("tensor_copy", [sob], [bf("krdup")], out=krdup[:Pt, :].rearrange("p (a d) -> p a d", a=2),
                            in_=so[:Pt, KVL:KVL + 64].unsqueeze(1).broadcast_to([Pt, 2, 64]))
                        for c in range(2):
                            tp(psh(6)[:, c * 128:c * 128 + Pt], ckvtv[:Pt, kti, c * 128:(c + 1) * 128], identb[:Pt, :Pt],
                               [bf("ckvt"), bf("identb")], [psb[6]])
                        tp(psh(6)[:, 256:256 + Pt], krdup[:Pt, :], identb[:Pt, :Pt], [bf("krdup"), bf("identb")], [psb[6]])
                        act("activation", [psb[6]], [bf("ckvT")], out=ckvTv[:, :, kpos],
                            in_=psh(6)[:, 0:256].rearrange("p (c t) -> p c t", c=2)[:, :, :Pt], func=AF.Copy)
                        act("activation", [psb[6]], [bf("krT")], out=krT[:, kpos], in_=psh(6)[:, 256:256 + Pt], func=AF.Copy)

                zb = 0
                pend = []
                for g in (6, 7, 2, 3, 4, 5, 0, 1):
                    ckpt("Bg")
                    wt, wb = wload(l, T_IN + g)
                    wv = wt.rearrange("p (k c) -> p k c", k=8)
                    ncol = IN_NCOLS[g]
                    for tt in range(nTT):
                        bi = zb % 6
                        zb += 1
                        zp = psf(bi)[:Pt, :ncol]
                        tks = slice(tt * Pt, (tt + 1) * Pt)
                        for kc in range(8):
                            mm(zp, hTv[:, kc, tks], wv[:, kc, :ncol], kc == 0, kc == 7, [hTb[tt], wb], [psb[bi]])
                        pend.append((g, tt, bi, zp))
                        if len(pend) > 2:
                            postB(*pend.pop(0))
                while pend:
                    postB(*pend.pop(0))

                ckpt("C")
                fence(CD_TMP, B_TMP)
                NL = lamt[:, 4 * l + 3:4 * l + 4]
                LNC = lamt[:, 4 * L + l:4 * L + l + 1]
                nkt = len(ktl)

                sbank = {}
                sctr = [0]
                pend_epi = []

                def da_qpad(h):
                    qb = bf("qdaT%d" % (h // 4))
                    qpb = bf("qpad%d" % (h % 2))
                    qpv = qpad[h % 2][:].rearrange("p (a n) -> p a n", a=2)
                    dve("tensor_copy", [qb], [qpb], out=qpv[0:64, 0, :N], in_=qdaTv[0:64, h, :N])
                    dve("tensor_copy", [qb], [qpb], out=qpv[64:128, 1, :N], in_=qdaTv[64:128, h, :N])

                def da_S(i):
                    h, ki = divmod(i, nkt)
                    kt, tk, q0, diag = ktl[ki]
                    qpb = bf("qpad%d" % (h % 2))
                    qpv = qpad[h % 2][:].rearrange("p (a n) -> p a n", a=2)
                    nq = N - q0
                    ksl = slice(kt * 128, kt * 128 + tk)
                    sbank[i] = (2 * (i % 2), 2 * (i % 2) + 1)
                    for a in range(2):
                        sb_ = sbank[i][a]
                        mm(psf(sb_)[:tk, :nq], kTv[:, h, ksl], qpv[:, a, q0:N], True, True, [bf("kT"), qpb], [psb[sb_]])
                    if ki == 0 and h + 1 < NH:
                        da_qpad(h + 1)

                def da_epi2(h, sb_):
                    mm(psf(sb_)[:, :N], onesm[:], sqb[:, :N], True, True, [bf("onesm"), bf("sqb")], [psb[sb_]])
                    act("activation", [psb[sb_], bf("epsc")], [bf("nr2")], out=nr2[:, :N], in_=psf(sb_)[:, :N], func=AF.Ln, bias=epsc)
                    act("activation", [bf("nr2"), LB], [bf("nr2")], out=nr2[:, :N], in_=nr2[:, :N], func=AF.Exp, scale=-0.5, bias=LNC)
                    dve("scalar_tensor_tensor", [bf("nr1"), bf("nr2"), bf("spt")], [bf("odaT%d" % h)], out=odaT[:, h, :N], in0=nr1[:, :N],
                        scalar=spt[:, spb + SP_GDA:spb + SP_GDA + 1], in1=nr2[:, :N], op0=ALU.mult, op1=ALU.mult)

                def da_PV(i):
                    h, ki = divmod(i, nkt)
                    kt, tk, q0, diag = ktl[ki]
                    nq = N - q0
                    for a in range(2):
                        sb_ = sbank[i][a]
                        pi = pctr[0] % NPS_
                        pctr[0] += 1
                        P = Psl[pi]
                        Pb = bf("P%d" % pi)
                        act("activation", [psb[sb_]], [Pb], out=P[:tk, :nq], in_=psf(sb_)[:tk, :nq], func=AF.Exp, scale=0.125)
                        if diag:
                            pool("memset", [], [Pb], ap=P[64:128, 0:64], constant=0.0)
                        mm(psf(4 + a)[:, q0:N], vdav[:tk, kt, h * 128:(h + 1) * 128], P[:tk, :nq], ki == 0, ki == nlast,
                           [bf("vda"), Pb], [psb[4 + a]])
                        mm(psf(6 + a)[:, q0:N], onesb[:tk, :], P[:tk, :nq], ki == 0, ki == nlast, [bf("onesb"), Pb], [psb[6 + a]])
                    if ki == nlast:
                        act("activation", [psb[6]], [bf("nr1")], out=nr1[:, :N], in_=psf(6)[:, :N], func=AF.Ln)
                        act("activation", [bf("nr1")], [bf("nr1")], out=nr1[:, :N], in_=nr1[:, :N], func=AF.Exp, scale=-1.0)
                        dve("tensor_tensor", [psb[4], bf("nr1")], [bf("nr1")], out=nr1[:, :N], in0=psf(4)[:, :N], in1=nr1[:, :N], op=ALU.mult)
                        act("activation", [psb[7]], [bf("nr2")], out=nr2[:, :N], in_=psf(7)[:, :N], func=AF.Ln)
                        act("activation", [bf("nr2")], [bf("nr2")], out=nr2[:, :N], in_=nr2[:, :N], func=AF.Exp, scale=-1.0)
                        dve("tensor_tensor", [psb[5], bf("nr2")], [bf("nr2")], out=nr2[:, :N], in0=psf(5)[:, :N], in1=nr2[:, :N], op=ALU.mult)
                        dve("scalar_tensor_tensor", [bf("nr1"), bf("nr2"), LB], [bf("nr1")], out=nr1[:, :N], in0=nr2[:, :N], scalar=NL,
                            in1=nr1[:, :N], op0=ALU.mult, op1=ALU.add)
                        pool("tensor_tensor", [bf("nr1")], [bf("sqb")], out=sqb[:, :N], in0=nr1[:, :N], in1=nr1[:, :N], op=ALU.mult)
                        pend_epi.append([2, h])

                def da_flush(i, force=False):
                    for e_ in list(pend_epi):
                        e_[0] -= 1
                        if e_[0] <= 0 or force:
                            da_epi2(e_[1], sbank[i][0])
                            pend_epi.remove(e_)

                load(rfm[:, :, :N], rt_fm[:, :, t0:t0 + N].rearrange("a p n -> p a n"), [], [bf("rfm")], "rfm")
                wt, wb = wload(l, T_UQR)
                wv = wt[:, 0:3072].rearrange("p (k v c) -> p k v c", k=3, v=2)
                for pr in range(4):
                    for v in range(2):
                        for kc in range(3):
                            mm(psf(6 + v)[:, :N], wv[:, kc, v, pr * 128:(pr + 1) * 128], cqT[:, kc, :N], kc == 0, kc == 2,
                               [wb, bf("cqT")], [psb[6 + v]])
                    dve("tensor_tensor", [psb[6], bf("rfm")], [bf("nr1")], out=nr1[:, :N], in0=psf(6)[:, :N], in1=rfm[:, 0, :N], op=ALU.mult)
                    dve("tensor_tensor", [psb[7], bf("rfm")], [bf("nr2")], out=nr2[:, :N], in0=psf(7)[:, :N], in1=rfm[:, 1, :N], op=ALU.mult)
                    pool("tensor_tensor", [bf("nr1"), bf("nr2")], [bf("qrT")], out=qrTv[0:64, 2 * pr, :N], in0=nr1[0:64, :N], in1=nr2[0:64, :N], op=ALU.add)
                    pool("tensor_tensor", [bf("nr1"), bf("nr2")], [bf("qrT")], out=qrTv[64:128, 2 * pr + 1, :N], in0=nr1[64:128, :N],
                         in1=nr2[64:128, :N], op=ALU.add)
                wtn, wbn = wload(l, T_UQN)
                wvn = wtn[:, 0:3072].rearrange("p (k c) -> p k c", k=3)
                wtk, wbk = wload(l, T_UKT)
                wvk = wtk[:, 0:2048].rearrange("p (h c) -> p h c", h=8)
                wvu = wtk[:, 2048:4096].rearrange("p (h c v) -> p h c v", h=8, c=2)

                qns = [qn, qn_b]
                qlats = [qlat, qlat_b]
                olats = [olat, olat_b]

                def mla_pro(h):
                    k = h % 2
                    for kc in range(3):
                        mm(psf(6)[:, :N], wvn[:, kc, h * 128:(h + 1) * 128], cqT[:, kc, :N], kc == 0, kc == 2, [wbn, bf("cqT")], [psb[6]])
                    dve("tensor_copy", [psb[6]], [bf("qn%d" % k)], out=qns[k][:, :N], in_=psf(6)[:, :N])
                    for c in range(2):
                        mm(psf(7)[:, :N], wvk[:, h, c * 128:(c + 1) * 128], qns[k][:, :N], True, True, [wbk, bf("qn%d" % k)], [psb[7]])
                        dve("tensor_copy", [psb[7]], [bf("qlat%d" % k)], out=qlats[k][:, c, :N], in_=psf(7)[:, :N])

                mla_pro(0)
                nsteps = NH * nkt
                da_qpad(0)
                da_S(0)
                for i in range(nsteps):
                    if i + 1 < nsteps:
                        da_S(i + 1)
                    da_PV(i)
                    da_flush(i)
                da_flush(nsteps - 1, True)

                ckpt("D")
                pend_m = []

                def mla_S(i):
                    h, ki = divmod(i, nkt)
                    kt, tk, q0, diag = ktl[ki]
                    k = h % 2
                    nq = N - q0
                    ksl = slice(kt * 128, kt * 128 + tk)
                    sk = i % 3
                    mm(psf(sk)[:tk, :nq], ckvTv[:, 0, ksl], qlats[k][:, 0, q0:N], True, False, [bf("ckvT"), bf("qlat%d" % k)], [psb[sk]])
                    mm(psf(sk)[:tk, :nq], ckvTv[:, 1, ksl], qlats[k][:, 1, q0:N], False, False, [bf("ckvT"), bf("qlat%d" % k)], [psb[sk]])
                    mm(psf(sk)[:tk, :nq], krT[:, ksl], qrTv[:, h, q0:N], False, True, [bf("krT"), bf("qrT")], [psb[sk]])
                    if ki == 0 and h + 1 < NH:
                        mla_pro(h + 1)

                def mla_epi2(h):
                    k = h % 2
                    for c in range(2):
                        mm(psf(6)[:, :N], wvu[:, h, c, :], olats[k][:, c, :N], c == 0, c == 1, [wbk, bf("olat%d_%d" % (c, k))], [psb[6]])
                    dve("tensor_tensor", [psb[6], bf("nr1")], [bf("omlaT%d" % h)], out=omlaT[:, h, :N], in0=psf(6)[:, :N],
                        in1=nr1[:, :N], op=ALU.mult)

                def mla_PV(i):
                    h, ki = divmod(i, nkt)
                    kt, tk, q0, diag = ktl[ki]
                    k = h % 2
                    nq = N - q0
                    sk = i % 3
                    pi = pctr[0] % NPS_
                    pctr[0] += 1
                    P = Psl[pi]
                    Pb = bf("P%d" % pi)
                    act("activation", [psb[sk]], [Pb], out=P[:tk, :nq], in_=psf(sk)[:tk, :nq], func=AF.Exp, scale=MLA_SCALE)
                    if diag:
                        pool("memset", [], [Pb], ap=P[64:128, 0:64], constant=0.0)
                    for c in range(2):
                        mm(psf(3 + c)[:, q0:N], ckvtv[:tk, kt, c * 128:(c + 1) * 128], P[:tk, :nq], ki == 0, ki == nlast,
                           [bf("ckvt"), Pb], [psb[3 + c]])
                    mm(psf(5)[:, q0:N], onesb[:tk, :], P[:tk, :nq], ki == 0, ki == nlast, [bf("onesb"), Pb], [psb[5]])
                    if ki == nlast:
                        dve("tensor_copy", [psb[3]], [bf("olat0_%d" % k)], out=olats[k][:, 0, :N], in_=psf(3)[:, :N])
                        dve("tensor_copy", [psb[4]], [bf("olat1_%d" % k)], out=olats[k][:, 1, :N], in_=psf(4)[:, :N])
                        act("activation", [psb[5]], [bf("nr1")], out=nr1[:, :N], in_=psf(5)[:, :N], func=AF.Ln)
                        act("activation", [bf("nr1")], [bf("nr1")], out=nr1[:, :N], in_=nr1[:, :N], func=AF.Exp, scale=-1.0)
                        pend_m.append([2, h])

                def mla_flush(force=False):
                    for e_ in list(pend_m):
                        e_[0] -= 1
                        if e_[0] <= 0 or force:
                            mla_epi2(e_[1])
                            pend_m.remove(e_)

                mla_S(0)
                for i in range(nsteps):
                    if i + 1 < nsteps:
                        mla_S(i + 1)
                    mla_PV(i)
                    mla_flush()
                mla_flush(True)

                ckpt("E")
                fence(R1_MT, ["cqT", "qn0", "qlat0", "olat0_0", "olat1_0"])
                for t in range(4):
                    wt, wb = wload(l, T_G + t)
                    wv = wt.rearrange("p (k c) -> p k c", k=8)
                    for i in range(2):
                        jch = 2 * t + i
                        for ab in range(2):
                            bi = (2 * jch + ab) % 4
                            for kc in range(8):
                                mm(psf(bi)[:, :N], wv[:, kc, (2 * ab + i) * 128:(2 * ab + i + 1) * 128], hTv[:, kc, :N], kc == 0, kc == 7,
                                   [wb] + hTb, [psb[bi]])
                            dst = nr1 if ab == 0 else nr2
                            bcol = spb + SP_BG + jch + 8 * ab
                            act("activation", [psb[bi], bf("spt")], [bf("nr1") if ab == 0 else bf("nr2")], out=dst[:, :N], in_=psf(bi)[:, :N],
                                func=AF.Sigmoid, bias=spt[:, bcol:bcol + 1])
                        dve("tensor_tensor", [bf("nr1"), bf("odaT%d" % jch)], [bf("nr1")], out=nr1[:, :N], in0=nr1[:, :N], in1=odaT[:, jch, :N], op=ALU.mult)
                        pool("tensor_tensor", [bf("nr2"), bf("omlaT%d" % jch)], [bf("nr2")], out=nr2[:, :N], in0=nr2[:, :N], in1=omlaT[:, jch, :N], op=ALU.mult)
                        dve("tensor_tensor", [bf("nr1"), bf("nr2")], [bf("mT%d" % jch)], out=mT[:, jch, :N], in0=nr1[:, :N], in1=nr2[:, :N], op=ALU.add)
                mTb = [bf(n) for n in R1_MT]
                for g in range(2):
                    wt, wb = wload(l, T_O + g)
                    wv = wt.rearrange("p (k c) -> p k c", k=8)
                    for tt in range(nTT):
                        bi = 4 + (g * nTT + tt) % 4
                        tks = slice(tt * Pt, (tt + 1) * Pt)
                        for kc in range(8):
                            mm(psf(bi)[:Pt, :], mT[:, kc, tks], wv[:, kc, :], kc == 0, kc == 7, [wb] + mTb, [psb[bi]])
                        dve("tensor_tensor", [psb[bi], xbb[tt]], [xbb[tt]], out=xbv[:Pt, tt, g * 512:(g + 1) * 512], in0=psf(bi)[:Pt, :],
                            in1=xbv[:Pt, tt, g * 512:(g + 1) * 512], op=ALU.add)

                ckpt("F")
                norm_hT(L + l)
                fence(FFN_TMP, CD_TMP + B_TMP)
                fence(R1_AT, R1_ATT + R1_MT)
                wc = spb + SP_WC

                def ffn_mm(it):
                    t, i = divmod(it, 2)
                    if i == 0:
                        wt_, wb_ = wload(l, T_UP + t)
                        fstate[0] = (wt_.rearrange("p (k c) -> p k c", k=8), wb_)
                    wv_, wb_ = fstate[0]
                    for ab in range(2):
                        bi = (2 * it + ab) % 4
                        for kc in range(8):
                            mm(psf(bi)[:, :N], wv_[:, kc, (2 * ab + i) * 128:(2 * ab + i + 1) * 128], hTv[:, kc, :N], kc == 0, kc == 7,
                               [wb_] + hTb, [psb[bi]])

                def ffn_post(it):
                    k = it % 2
                    ca = it
                    for ab in range(2):
                        ch = ca + 22 * ab
                        bi = (2 * it + ab) % 4
                        ue, ueb_ = ues[ab][k], bf("ue%d_%d" % (ab, k))
                        cc, ccb_ = ccs[ab][k], bf("cc%d_%d" % (ab, k))
                        pool("tensor_copy", [bf("carry")], [ueb_], out=ue[:, 0:2], in_=carv[:, :, ch])
                        act("activation", [psb[bi]], [ueb_], out=ue[:, 2:2 + N], in_=psf(bi)[:, :N], func=AF.Copy)
                        act("activation", [psb[bi], bf("spt")], [ccb_], out=cc[:, :N], in_=psf(bi)[:, :N], func=AF.Identity,
                            scale=spt[:, wc + 2 * NCH + ch:wc + 2 * NCH + ch + 1], bias=spt[:, spb + SP_BC + ch:spb + SP_BC + ch + 1])
                        dve("scalar_tensor_tensor", [ueb_, ccb_, bf("spt")], [ccb_], out=cc[:, :N], in0=ue[:, 1:1 + N],
                            scalar=spt[:, wc + NCH + ch:wc + NCH + ch + 1], in1=cc[:, :N], op0=ALU.mult, op1=ALU.add)
                        dve("scalar_tensor_tensor", [ueb_, ccb_, bf("spt")], [ccb_], out=cc[:, :N], in0=ue[:, 0:N],
                            scalar=spt[:, wc + ch:wc + ch + 1], in1=cc[:, :N], op0=ALU.mult, op1=ALU.add)
                        pool("tensor_copy", [ueb_], [bf("carry")], out=carv[:, :, ch], in_=ue[:, N:N + 2])
                    ca_, cb_ = ccs[0][k], ccs[1][k]
                    act("activation", [bf("cc0_%d" % k)], [bf("cc0_%d" % k)], out=ca_[:, :N], in_=ca_[:, :N], func=AF.Silu)
                    dve("tensor_tensor", [bf("cc0_%d" % k), bf("cc1_%d" % k)], [bf("aT%d" % ca)], out=aT[:, ca, :N], in0=ca_[:, :N], in1=cb_[:, :N], op=ALU.mult)

                fstate = [None]
                ffn_mm(0)
                for it in range(22):
                    if it + 1 < 22:
                        ffn_mm(it + 1)
                    ffn_post(it)
                aTb = [bf(n) for n in R1_AT]
                for g in range(2):
                    for ks in range(3):
                        wt, wb = wload(l, T_DN + g * 3 + ks)
                        wv = wt.rearrange("p (k c) -> p k c", k=8)
                        nk = 8 if ks < 2 else 6
                        for tt in range(nTT):
                            bi = 4 + tt
                            tks = slice(tt * Pt, (tt + 1) * Pt)
                            for kk in range(nk):
                                kc = ks * 8 + kk
                                mm(psf(bi)[:Pt, :], aT[:, kc, tks], wv[:, kk, :], kc == 0, kc == 21, [wb] + aTb, [psb[bi]])
                    for tt in range(nTT):
                        bi = 4 + tt
                        dve("tensor_tensor", [psb[bi], xbb[tt]], [xbb[tt]], out=xbv[:Pt, tt, g * 512:(g + 1) * 512], in0=psf(bi)[:Pt, :],
                            in1=xbv[:Pt, tt, g * 512:(g + 1) * 512], op=ALU.add)

                ckpt("G")
                if l < L - 1:
                    for tt in range(nTT):
                        rows = slice(t0 + tt * Pt, t0 + (tt + 1) * Pt)
                        store(x1d[rows, :], xbv[:Pt, tt, :], [xbb[tt]], "xst%d" % tt, w=[bf("x1d%d_%d" % (j, tt))])
                else:
                    load(gbc[:], bcd[2 * L], [], [bf("gbc"), bf("gbc2")], "gbc")
                    for tt in range(nTT):
                        rows = slice(t0 + tt * Pt, t0 + (tt + 1) * Pt)
                        rs, rsb = rms_stats(xbv[:Pt, tt, :], Pt, D, [xbb[tt]], hns[tt % 2][:Pt, :], bf("hn%d" % (tt % 2)))
                        dve("scalar_tensor_tensor", [xbb[tt], rsb, bf("gbc"), bf("gbc2")], [xbb[tt]], out=xbv[:Pt, tt, :], in0=xbv[:Pt, tt, :],
                            scalar=rs[:Pt, :], in1=gbc[:Pt, :], op0=ALU.mult, op1=ALU.mult)
                        store(y_out[rows, :], xbv[:Pt, tt, :], [xbb[tt]], "xst%d" % tt)
                if j == NB - 1:
                    tp(psf(0)[:88, :128], carry[:, :], identf[:, :], [bf("carry"), bf("identf")], [psb[0]])
                    dve("tensor_copy", [psb[0]], [bf("stgc")], out=stgc[:88, :], in_=psf(0)[:88, :128])
                    store(o_cv[grp][l, si].rearrange("t (c p) -> (t c) p", p=128), stgc[:88, :], [bf("stgc")], "stgc_out")

    try:
        for si in range(NPS):
            run_sequence(0, si, S, 128, 0)
        for si in range(NSS):
            run_sequence(1, si, SD, SD, PAST)
    except _Stop:
        pass

    for e in R.ENGS:
        for o in R.ops[e]:
            for d in o.deps:
                if not d.dma:
                    d.sig = True
    cnt = {e: 0 for e in R.ENGS}
    for e in R.ENGS:
        for o in R.ops[e]:
            if (not o.dma) and o.sig:
                cnt[e] += 1
                o.val = cnt[e]
    tag_total = {t: 16 * len(v) for t, v in R.tag_ops.items()}
    sems = {}
    for e in ("pe", "act", "dve", "pool"):
        sems[e] = nc.alloc_semaphore("s_" + e)
    for t in R.tag_ops:
        sems["t_" + t] = nc.alloc_semaphore("t_" + t)

    def sigof(d):
        if d.dma:
            v = tag_total[d.tag] if d.tag in R.wait_all_tags else d.val
            return "t_" + d.tag, v
        return d.eng, d.val

    engh = {"pe": "tensor", "act": "scalar", "dve": "vector", "pool": "gpsimd", "sp": "sync"}

    def emit(engname, eng):
        known = {}
        for o in R.ops[engname]:
            need = {}
            for d in o.deps:
                s, v = sigof(d)
                if known.get(s, 0) < v and need.get(s, 0) < v:
                    need[s] = v
            for s, v in need.items():
                eng.wait_ge(sems[s], v)
                known[s] = v
            ins = o.fn(eng)
            if o.dma:
                ins.then_inc(sems["t_" + o.tag], 16)
            elif o.sig:
                ins.then_inc(sems[engname], 1)
        if engname == "sp":
            fin = {}
            for d in R.final_deps:
                s, v = sigof(d)
                fin[s] = max(fin.get(s, 0), v)
            for s, v in fin.items():
                eng.wait_ge(sems[s], v)

    with nc.Block() as block:
        @block.tensor
        def _(e):
            emit("pe", e)

        @block.scalar
        def _(e):
            emit("act", e)

        @block.vector
        def _(e):
            emit("dve", e)

        @block.gpsimd
        def _(e):
            emit("pool", e)

        @block.sync
        def _(e):
            emit("sp", e)
    es.close()
    nops = {e: len(R.ops[e]) for e in R.ENGS}
    return nc, nops


def _pack_weights(w_in, w_uq, w_uk, w_uv, w_o, w_up, w_down, L):
    wp = np.zeros((L, NT, 128, 4096), np.float32)
    for l in range(L):
        wi = w_in[l]
        colsets = [np.arange(g * 512, (g + 1) * 512) for g in range(6)]
        colsets.append(np.arange(3072, 3456))
        colsets.append(np.arange(3456, 3776))
        for g, cs in enumerate(colsets):
            t = wi[:, cs].reshape(8, 128, len(cs)).transpose(1, 0, 2)
            buf = np.zeros((128, 8, 512), np.float32)
            buf[:, :, :len(cs)] = t
            wp[l, T_IN + g] = buf.reshape(128, 4096)
        uq = w_uq[l]
        t = uq[:, :, :128].reshape(3, 128, 8 * 128).transpose(1, 0, 2)
        wp[l, T_UQN, :, :3072] = t.reshape(128, 3072)
        rp = uq[:, :, 128:]
        swap = np.concatenate([np.arange(32, 64), np.arange(0, 32)])
        r0 = rp.reshape(3, 128, 512)
        r1 = rp[:, :, swap].reshape(3, 128, 512)
        t = np.stack([r0, r1], axis=2).transpose(1, 0, 2, 3)
        wp[l, T_UQR, :, :3072] = t.reshape(128, 3072)
        wp[l, T_UKT, :, :2048] = w_uk[l].transpose(2, 0, 1).reshape(128, 2048)
        t = w_uv[l].reshape(8, 2, 128, 128).transpose(2, 0, 1, 3)
        wp[l, T_UKT, :, 2048:4096] = t.reshape(128, 2048)
        for tt in range(4):
            cs = np.concatenate([3776 + np.arange((2 * tt) * 128, (2 * tt + 2) * 128),
                                 3776 + 1024 + np.arange((2 * tt) * 128, (2 * tt + 2) * 128)])
            wp[l, T_G + tt] = wi[:, cs].reshape(8, 128, 512).transpose(1, 0, 2).reshape(128, 4096)
        for g in range(2):
            wp[l, T_O + g] = w_o[l][:, g * 512:(g + 1) * 512].reshape(8, 128, 512).transpose(1, 0, 2).reshape(128, 4096)
        for tt in range(11):
            cs = np.concatenate([np.arange(2 * tt * 128, (2 * tt + 2) * 128), DFF + np.arange(2 * tt * 128, (2 * tt + 2) * 128)])
            wp[l, T_UP + tt] = w_up[l][:, cs].reshape(8, 128, 512).transpose(1, 0, 2).reshape(128, 4096)
        for g in range(2):
            for ks in range(3):
                nk = 8 if ks < 2 else 6
                rows = slice(ks * 1024, ks * 1024 + nk * 128)
                t = w_down[l][rows, g * 512:(g + 1) * 512].reshape(nk, 128, 512).transpose(1, 0, 2)
                buf = np.zeros((128, 8, 512), np.float32)
                buf[:, :nk] = t
                wp[l, T_DN + g * 3 + ks] = buf.reshape(128, 4096)
    return wp


def _pack_small(b_gate, g_cq, g_da_head, w_conv, b_conv, lam_q1, lam_k1, lam_q2, lam_k2, g_ckv, L):
    sp = np.zeros((128, L * SPW), np.float32)
    for l in range(L):
        b = l * SPW
        sp[:, b + SP_BG:b + SP_BG + 16] = b_gate[l].reshape(16, 128).T
        sp[:, b + SP_GCQ:b + SP_GCQ + 3] = g_cq[l].reshape(3, 128).T
        sp[:, b + SP_GDA] = g_da_head[l]
        for j in range(3):
            sp[:, b + SP_WC + j * NCH:b + SP_WC + (j + 1) * NCH] = w_conv[l, j].reshape(NCH, 128).T
        sp[:, b + SP_BC:b + SP_BC + NCH] = b_conv[l].reshape(NCH, 128).T
        for j, v in enumerate((lam_q1, lam_k1, lam_q2, lam_k2)):
            sp[:, b + SP_LAM + 64 * j:b + SP_LAM + 64 * (j + 1)] = v[l][None, :]
        sp[:, b + SP_GCKV:b + SP_GCKV + KVL] = g_ckv[l][None, :]
    return sp


def _rope_tables(pos):
    half = 32
    inv = np.power(np.float32(10000.0), -np.arange(half, dtype=np.float32) / np.float32(half)).astype(np.float32)
    ang = (pos.astype(np.float32)[:, None] * inv[None, :]).astype(np.float32)
    c = np.cos(ang).astype(np.float32)
    s = np.sin(ang).astype(np.float32)
    n = len(pos)
    tok = np.zeros((n, 2, 64), np.float32)
    tok[:, 0, :32] = c
    tok[:, 0, 32:] = c
    tok[:, 1, :32] = s
    tok[:, 1, 32:] = -s
    fm = np.zeros((2, 128, n), np.float32)
    for p in range(128):
        fm[0, p] = c[:, p % 32]
        fm[1, p] = (-s[:, p % 32]) if (p % 64) < 32 else s[:, p % 32]
    return tok, fm


_CACHE = {}


def run(cfg, inputs, n_cores):
    L, S, NPS, NSS, SD, PAST = cfg.L, cfg.S, cfg.NPS, cfg.NSS, cfg.SD, cfg.PAST
    f = lambda a: np.ascontiguousarray(np.asarray(a, dtype=np.float32))
    I = {k: f(v) for k, v in inputs.items()}
    key = (L, S, NPS, NSS, SD, PAST)
    if key not in _CACHE:
        _CACHE[key] = build(cfg)
    nc, nops = _CACHE[key]
    wp = _pack_weights(I["w_in"], I["w_uq"], I["w_uk"], I["w_uv"], I["w_o"], I["w_up"], I["w_down"], L)
    sp = _pack_small(I["b_gate"], I["g_cq"], I["g_da_head"], I["w_conv"], I["b_conv"], I["lam_q1"], I["lam_k1"],
                     I["lam_q2"], I["lam_k2"], I["g_ckv"], L)
    bcd = np.zeros((2 * L + 1, 128, D), np.float32)
    for l in range(L):
        bcd[l] = I["g_attn"][l][None, :]
        bcd[L + l] = I["g_ffn"][l][None, :]
    bcd[2 * L] = I["g_final"][None, :]
    rtp, rfp = _rope_tables(np.arange(S))
    rts, rfs = _rope_tables(PAST + np.arange(SD))
    ident = np.eye(128, dtype=np.float32)
    in_maps = []
    for c in range(n_cores):
        ps_ = slice(c * NPS, (c + 1) * NPS)
        ss_ = slice(c * NSS, (c + 1) * NSS)
        in_maps.append({
            "xp": I["x_prompt"][ps_], "xs": I["x_sample"][ss_],
            "cdk": np.ascontiguousarray(I["cache_dk"][:, ss_].reshape(L, NSS, PAST, 1024)),
            "cdv": np.ascontiguousarray(I["cache_dv"][:, ss_].reshape(L, NSS, PAST, 1024)),
            "cckv": np.ascontiguousarray(I["cache_ckv"][:, ss_]),
            "ckr": np.ascontiguousarray(I["cache_krope"][:, ss_]),
            "sconv": np.ascontiguousarray(I["state_conv"][:, ss_]),
            "wpack": wp, "spd": sp, "bcd": bcd, "rtp": rtp, "rts": rts, "rfp": rfp, "rfs": rfs, "identd": ident,
        })
    res = run_bass_kernel_spmd(nc, in_maps, core_ids=list(range(n_cores)))
    rs = res.results
    cat0 = lambda k: np.concatenate([r[k] for r in rs], axis=0)
    cat1 = lambda k: np.concatenate([r[k] for r in rs], axis=1)
    Bp, Bs = NPS * n_cores, NSS * n_cores
    return (cat0("o_yp"), cat0("o_ys"),
            cat1("o_dkp").reshape(L, Bp, S, NH, 128), cat1("o_dvp").reshape(L, Bp, S, NH, 128),
            cat1("o_ckvp"), cat1("o_krp"), cat1("o_cvp"),
            cat1("o_dks").reshape(L, Bs, SD, NH, 128), cat1("o_dvs").reshape(L, Bs, SD, NH, 128),
            cat1("o_ckvs"), cat1("o_krs"), cat1("o_cvs"))


def kernel(**inputs):
    cfg = Cfg(L=2, S=2048, NPS=4, NSS=2, SD=32, PAST=2048)
    return run(cfg, inputs, 8)
```

```python
import math
from contextlib import ExitStack

import numpy as np
import concourse.bass as bass
import concourse.mybir as mybir
from concourse.bass_utils import run_bass_kernel_spmd

F32 = mybir.dt.float32
BF16 = mybir.dt.bfloat16
AF = mybir.ActivationFunctionType
ALU = mybir.AluOpType
AX = mybir.AxisListType

D = 1024
NH = 8
QL = 384
KVL = 256
RD = 64
DFF = 2816
NCH = 44
EPS = 1e-6
CHUNK = 64
MLA_SCALE = (128 + 64) ** -0.5
NT = 35
SPW = 708
SP_BG, SP_GCQ, SP_GDA, SP_WC, SP_BC, SP_LAM, SP_GCKV = 0, 16, 19, 20, 152, 196, 452
T_IN, T_UQN, T_UQR, T_UKT, T_UV, T_G, T_O, T_UP, T_DN = 0, 8, 9, 10, 11, 12, 16, 18, 29
IN_NCOLS = [512] * 6 + [384, 320]


class Buf:
    __slots__ = ("name", "w", "r", "excl")

    def __init__(self, name, excl=False):
        self.name = name
        self.w = None
        self.r = []
        self.excl = excl


class Op:
    __slots__ = ("eng", "fn", "deps", "dma", "sig", "val", "tag", "users")

    def __init__(self, eng, fn, dma, tag):
        self.eng = eng
        self.fn = fn
        self.deps = []
        self.dma = dma
        self.sig = False
        self.val = 0
        self.tag = tag
        self.users = 0


class Rec:
    ENGS = ("pe", "act", "dve", "pool", "sp")

    def __init__(self):
        self.ops = {e: [] for e in self.ENGS}
        self.tag_ops = {}
        self.wait_all_tags = set()
        self.final_deps = []

    def add(self, eng, fn, reads=(), writes=(), dma=False, tag=None):
        o = Op(eng, fn, dma, tag)
        deps = {}
        for b in reads:
            if b.w is not None:
                deps[id(b.w)] = b.w
            if b.excl:
                for r in b.r:
                    if r.eng != eng:
                        deps[id(r)] = r
        for b in writes:
            if b.w is not None:
                deps[id(b.w)] = b.w
            for r in b.r:
                deps[id(r)] = r
        for b in reads:
            if not dma:
                b.r = [r for r in b.r if r.dma or r.eng != eng]
            b.r.append(o)
        for b in writes:
            b.w = o
            b.r = []
        for d in deps.values():
            if d is o:
                continue
            if (not d.dma) and (not dma) and d.eng == "pe" and eng == "pe":
                continue
            o.deps.append(d)
            d.users += 1
        if dma:
            lst = self.tag_ops.setdefault(tag, [])
            lst.append(o)
            o.val = 16 * len(lst)
        self.ops[eng].append(o)
        return o


class Cfg:
    def __init__(self, L=2, S=2048, NPS=4, NSS=2, SD=32, PAST=2048):
        self.L, self.S, self.NPS, self.NSS, self.SD, self.PAST = L, S, NPS, NSS, SD, PAST


def build(cfg):
    L, S, NPS, NSS, SD, PAST = cfg.L, cfg.S, cfg.NPS, cfg.NSS, cfg.SD, cfg.PAST
    NTOK = max(S, PAST + SD)
    NKT = (NTOK + 127) // 128
    nc = bass.Bass("TRN2", target_bir_lowering=False)
    R = Rec()

    def din(name, shape, dt=F32):
        return nc.dram_tensor(name, list(shape), dt, kind="ExternalInput").ap()

    def dout(name, shape):
        return nc.dram_tensor(name, list(shape), F32, kind="ExternalOutput").ap()

    xp = din("xp", [NPS, S, D])
    xs = din("xs", [NSS, SD, D])
    cdk = din("cdk", [L, NSS, PAST, 1024])
    cdv = din("cdv", [L, NSS, PAST, 1024])
    cckv = din("cckv", [L, NSS, PAST, KVL])
    ckr = din("ckr", [L, NSS, PAST, RD])
    sconv = din("sconv", [L, NSS, 2, 2 * DFF])
    wpack = din("wpack", [L, NT, 128, 4096])
    spd = din("spd", [128, L * SPW])
    bcd = din("bcd", [2 * L + 1, 128, D])
    rtp = din("rtp", [S, 2, 64])
    rts = din("rts", [SD, 2, 64])
    rfp = din("rfp", [2, 128, S])
    rfs = din("rfs", [2, 128, SD])
    identd = din("identd", [128, 128])

    o_yp = dout("o_yp", [NPS, S, D])
    o_ys = dout("o_ys", [NSS, SD, D])
    o_dk = [dout("o_dkp", [L, NPS, S, 1024]), dout("o_dks", [L, NSS, SD, 1024])]
    o_dv = [dout("o_dvp", [L, NPS, S, 1024]), dout("o_dvs", [L, NSS, SD, 1024])]
    o_ckv = [dout("o_ckvp", [L, NPS, S, KVL]), dout("o_ckvs", [L, NSS, SD, KVL])]
    o_kr = [dout("o_krp", [L, NPS, S, RD]), dout("o_krs", [L, NSS, SD, RD])]
    o_cv = [dout("o_cvp", [L, NPS, 2, 2 * DFF]), dout("o_cvs", [L, NSS, 2, 2 * DFF])]

    wscr = nc.dram_tensor("wscr", [L, NT, 128, 4096], BF16, kind="Internal").ap()
    x1d = nc.dram_tensor("x1d", [max(S, SD), D], F32, kind="Internal").ap()

    es = ExitStack()

    def sb(name, cols, dt=F32):
        return es.enter_context(nc.sbuf_tensor(name, [128, cols], dt))

    kT = sb("kT", NH * NTOK, BF16)
    kTv = kT[:].rearrange("p (h t) -> p h t", h=NH)
    vda = sb("vda", NKT * 1024, BF16)
    vdav = vda[:].rearrange("p (k c) -> p k c", k=NKT)
    ckvT = sb("ckvT", 2 * NTOK, BF16)
    ckvTv = ckvT[:].rearrange("p (c t) -> p c t", c=2)
    ckvt = sb("ckvt", NKT * KVL, BF16)
    ckvtv = ckvt[:].rearrange("p (k c) -> p k c", k=NKT)
    krT = sb("krT", NTOK, BF16)
    xb = sb("xb", 4 * D, F32)
    xbv = xb[:].rearrange("p (t d) -> p t d", t=4)
    hT = sb("hT", 8 * 512, BF16)
    hTv = hT[:].rearrange("p (k n) -> p k n", k=8)
    hns = [sb("hn0", D, BF16), sb("hn1", D, BF16)]
    gbc = sb("gbc", D, F32)
    wsl = [sb("wsl%d" % i, 4096, BF16) for i in range(2)]
    qdaT = sb("qdaT", 8 * 512, BF16)
    qdaTv = qdaT[:].rearrange("p (h n) -> p h n", h=8)
    qpad = [sb("qpad%d" % i, 2 * 512, BF16) for i in range(2)]
    r1 = sb("r1", 24576 // 2, BF16)
    odaT = r1[:, 0:4096].rearrange("p (h n) -> p h n", h=8)
    omlaT = r1[:, 4096:8192].rearrange("p (h n) -> p h n", h=8)
    cqT = r1[:, 8192:8192 + 1536].rearrange("p (k n) -> p k n", k=3)
    qn = r1[:, 9728:9728 + 512]
    qlat = r1[:, 10240:10240 + 1024].rearrange("p (c n) -> p c n", c=2)
    olat = r1[:, 11264:11264 + 1024].rearrange("p (c n) -> p c n", c=2)
    mT = r1[:, 8192:12288].rearrange("p (h n) -> p h n", h=8)
    aT = r1[:, 0:22 * 512].rearrange("p (c n) -> p c n", c=22)
    qrT = sb("qrT", 8 * 512, BF16)
    qrTv = qrT[:].rearrange("p (h n) -> p h n", h=8)
    r2 = sb("r2", 20480 // 4, F32)
    stg1s = [r2[:, 0:512], r2[:, 512:1024]]
    stg2s = [r2[:, 1024:1536], r2[:, 1536:2048]]
    stgos = [r2[:, 2048:2560], r2[:, 2560:3072]]
    rtok = r2[:, 3072:3584].rearrange("p (t a d) -> p t a d", t=4, a=2)
    r2bB = r2[:, 3584:4352].bitcast(BF16)
    qtoks = [r2bB[:, 0:512], r2bB[:, 512:1024]]
    cqn = r2bB[:, 1024:1408]
    krdup = r2bB[:, 1408:1536]
    stg1, stg2, stgo, qtok = stg1s[0], stg2s[0], stgos[0], qtoks[0]
    nr1 = r2[:, 0:512]
    nr2 = r2[:, 512:1024]
    rfm = r2[:, 1024:2048].rearrange("p (a n) -> p a n", a=2)
    r2bC = r2[:, 2048:5120].bitcast(BF16)
    NPS_ = 6
    Psl = [r2bC[:, i * 512:(i + 1) * 512] for i in range(NPS_)]
    sqb = r2bC[:, 3072:3584]
    qn_b = r2bC[:, 3584:4096]
    qlat_b = r2bC[:, 4096:5120].rearrange("p (c n) -> p c n", c=2)
    olat_b = r2bC[:, 5120:6144].rearrange("p (c n) -> p c n", c=2)
    ues = [[r2[:, (2 * sd + k) * 514:(2 * sd + k + 1) * 514] for k in range(2)] for sd in range(2)]
    ccs = [[r2[:, 2056 + (2 * sd + k) * 512:2056 + (2 * sd + k + 1) * 512] for k in range(2)] for sd in range(2)]
    spt = sb("spt", L * SPW, F32)
    identf = sb("identf", 128, F32)
    identb = sb("identb", 128, BF16)
    onesb = sb("onesb", 128, BF16)
    onesm = sb("onesm", 128, BF16)
    carry = sb("carry", 88, F32)
    carv = carry[:].rearrange("p (t c) -> p t c", t=2)
    stgc = sb("stgc", 128, F32)
    stat = sb("stat", 64, F32)
    lamt = sb("lamt", 8 * L + 8, F32)

    ps = [es.enter_context(nc.psum_tensor("ps%d" % i, [128, 512], F32)) for i in range(8)]
    psb = [Buf("ps%d" % i, excl=True) for i in range(8)]

    def psf(i):
        return ps[i][:]

    def psh(i):
        return ps[i][:].bitcast(BF16)

    B = {}

    def bf(name):
        if name not in B:
            B[name] = Buf(name)
        return B[name]

    wslb = [bf("wsl0"), bf("wsl1")]
    wctr = [0]

    def pe(fn, r, w):
        return R.add("pe", fn, r, w)

    unread = {}

    def _op(eng, meth, r, w, kw):
        for rb in r:
            if rb.excl:
                unread[rb.name] = None
        return R.add(eng, lambda e: getattr(e, meth)(**kw), r, w)

    def act(meth, r, w, **kw):
        return _op("act", meth, r, w, kw)

    def dve(meth, r, w, **kw):
        return _op("dve", meth, r, w, kw)

    def pool(meth, r, w, **kw):
        return _op("pool", meth, r, w, kw)

    def fence(new_names, old_names):
        ops = []
        for n in old_names:
            b = bf(n)
            if b.w is not None:
                ops.append(b.w)
            ops += b.r
        for n in new_names:
            nb = bf(n)
            nb.r = nb.r + ops

    def load(out_ap, in_ap, r, w, tag):
        return R.add("sp", lambda e: e.dma_start(out=out_ap, in_=in_ap), r, w, dma=True, tag=tag)

    def store(out_ap, in_ap, r, tag, w=(), q="pool"):
        o = R.add(q, lambda e: e.dma_start(out=out_ap, in_=in_ap), r, w, dma=True, tag=tag)
        R.final_deps.append(o)
        return o

    def mm(out, lhsT, rhs, start, stop, r, w):
        if start and unread.get(w[0].name) == "mm":
            raise AssertionError("PSUM bank %s overwritten before being read" % w[0].name)
        if stop:
            unread[w[0].name] = "mm"
        return pe(lambda e: e.matmul(out, lhsT=lhsT, rhs=rhs, start=start, stop=stop), r, w)

    def tp(out, in_, ident, r, w):
        unread[w[0].name] = "tp"
        return pe(lambda e: e.transpose(out, in_, ident), r, w)

    def wload(l, t):
        i = wctr[0] % 2
        wctr[0] += 1
        load(wsl[i][:], wscr[l, t], [bf("wscr%d_%d" % (l, t))], [wslb[i]], "wsl%d" % i)
        return wsl[i][:], wslb[i]

    load(identf[:], identd[:, :], [], [bf("identf")], "c_ident")
    load(spt[:], spd[:, :], [], [bf("spt")], "c_spt")
    dve("tensor_copy", [bf("identf")], [bf("identb")], out=identb[:], in_=identf[:])
    dve("memset", [], [bf("onesb")], ap=onesb[:], constant=1.0)
    dve("memset", [], [bf("onesm")], ap=onesm[:], constant=1.0 / 128.0)
    dve("memset", [], [bf("epsc")], ap=stat[:, 0:1], constant=EPS)
    pool("memset", [], [bf("qpad0")], ap=qpad[0][:], constant=0.0)
    pool("memset", [], [bf("qpad1")], ap=qpad[1][:], constant=0.0)
    pool("memset", [], [bf("qrT")], ap=qrT[:], constant=0.0)
    epsc = stat[:, 0:1]
    for l in range(L):
        for t in range(NT):
            if t == T_UV:
                continue
            R.add("pool", (lambda o_, i_: (lambda e: e.dma_start(out=o_, in_=i_)))(wscr[l, t], wpack[l, t]),
                  [], [bf("wscr%d_%d" % (l, t))], dma=True, tag="cast%d" % l)
        R.wait_all_tags.add("cast%d" % l)
    lam_init = [0.8 - 0.6 * math.exp(-0.3 * l) for l in range(L)]
    LB = bf("lamt")
    for l in range(L):
        base = l * SPW + SP_LAM
        for j in range(2):
            dve("tensor_tensor", [bf("spt")], [bf("stg1_0")], out=stg1[:, 0:64], in0=spt[:, base + 128 * j: base + 128 * j + 64],
                in1=spt[:, base + 128 * j + 64: base + 128 * j + 128], op=ALU.mult)
            dve("tensor_reduce", [bf("stg1_0")], [LB], out=lamt[:, 4 * l + j:4 * l + j + 1], in_=stg1[:, 0:64], axis=AX.X, op=ALU.add)
        act("activation", [LB], [LB], out=lamt[:, 4 * l:4 * l + 2], in_=lamt[:, 4 * l:4 * l + 2], func=AF.Exp)
        dve("tensor_tensor", [LB], [LB], out=lamt[:, 4 * l + 2:4 * l + 3], in0=lamt[:, 4 * l + 1:4 * l + 2],
            in1=lamt[:, 4 * l:4 * l + 1], op=ALU.subtract)
        dve("tensor_scalar", [LB], [LB], out=lamt[:, 4 * l + 3:4 * l + 4], in0=lamt[:, 4 * l + 2:4 * l + 3],
            scalar1=-lam_init[l], scalar2=None, op0=ALU.add)
        dve("memset", [], [LB], ap=lamt[:, 4 * L + l:4 * L + l + 1], constant=math.log(1.0 - lam_init[l]))

    import os as _os
    _stop = _os.environ.get("KSTOP", "")
    _ck = [0]

    class _Stop(Exception):
        pass

    def ckpt(name):
        _ck[0] += 1
        if _stop and _stop == "%s@%d" % (name, _ck[0]):
            raise _Stop()
        if _os.environ.get("KCKPT"):
            print("ckpt %s@%d" % (name, _ck[0]), {e: len(R.ops[e]) for e in R.ENGS})

    stat_ctr = [0]

    def rms_stats(in_ap, Pt, ncols, rbufs, junk_ap, junk_buf):
        stat_ctr[0] += 1
        k = stat_ctr[0] % 20
        c = 1 + k * 3
        ss, lnv, rs, sbuf_ = stat[:, c:c + 1], stat[:, c + 1:c + 2], stat[:, c + 2:c + 3], bf("stat%d" % k)
        act("activation", rbufs, [junk_buf, sbuf_], out=junk_ap, in_=in_ap, func=AF.Square, accum_out=ss[:Pt, :])
        act("activation", [sbuf_, bf("epsc")], [sbuf_], out=lnv[:Pt, :], in_=ss[:Pt, :], func=AF.Ln, scale=1.0 / ncols, bias=epsc[:Pt, :])
        act("activation", [sbuf_], [sbuf_], out=rs[:Pt, :], in_=lnv[:Pt, :], func=AF.Exp, scale=-0.5)
        return rs, sbuf_

    B_TMP = ["stg1_0", "stg1_1", "stg2_0", "stg2_1", "stgo_0", "stgo_1", "rtok", "qtok_0", "qtok_1", "cqn", "krdup"]
    CD_TMP = ["nr1", "nr2", "rfm", "sqb", "qn1", "qlat1", "olat0_1", "olat1_1"] + ["P%d" % i for i in range(NPS_)]
    FFN_TMP = ["ue%d_%d" % (sd, k) for sd in range(2) for k in range(2)] + ["cc%d_%d" % (sd, k) for sd in range(2) for k in range(2)]
    R1_ATT = ["odaT%d" % h for h in range(8)] + ["omlaT%d" % h for h in range(8)] + ["cqT", "qn0", "qlat0", "olat0_0", "olat1_0"]
    R1_MT = ["mT%d" % h for h in range(8)]
    R1_AT = ["aT%d" % c for c in range(22)]
    pctr = [0]
    rot = [0, 0, 0]

    def run_sequence(grp, si, Sq, Pt, past):
        NB = (Sq + 511) // 512
        kt_base0 = past // 128
        x_in = xp[si] if grp == 0 else xs[si]
        y_out = o_yp[si] if grp == 0 else o_ys[si]
        rt_tok = rtp if grp == 0 else rts
        rt_fm = rfp if grp == 0 else rfs
        for l in range(L):
            spb = l * SPW
            fence(B_TMP, FFN_TMP + CD_TMP)
            fence(R1_ATT, R1_AT + R1_MT)
            if past == 0:
                pool("memset", [], [bf("carry")], ap=carry[:], constant=0.0)
            else:
                load(stgc[:88, :], sconv[l, si].rearrange("t (c p) -> (t c) p", p=128), [], [bf("stgc")], "stgc_in")
                tp(psf(7)[:, 0:88], stgc[:88, :], identf[:88, :88], [bf("stgc"), bf("identf")], [psb[7]])
                dve("tensor_copy", [psb[7]], [bf("carry")], out=carry[:], in_=psf(7)[:, 0:88])
                fc = 0
                for kt in range(past // 128):
                    tsl = slice(kt * 128, (kt + 1) * 128)
                    for half in range(2):
                        a = fc % 2
                        fc += 1
                        kst, kstb, ktag = (gbc[:, 0:512], bf("gbc"), "gbc") if a == 0 else (stgos[0], bf("stgo_0"), "fill_k1")
                        vst, vstb, vtag = (gbc[:, 512:1024], bf("gbc2"), "gbc2") if a == 0 else (stgos[1], bf("stgo_1"), "fill_v1")
                        qt, qtb = qtoks[a], bf("qtok_%d" % a)
                        pb = 6 + a
                        load(kst, cdk[l, si, tsl, half * 512:(half + 1) * 512], [], [kstb], ktag)
                        dve("tensor_copy", [kstb], [qtb], out=qt, in_=kst)
                        for hh in range(4):
                            tp(psh(pb)[:, hh * 128:(hh + 1) * 128], qt[:, hh * 128:(hh + 1) * 128], identb[:],
                               [qtb, bf("identb")], [psb[pb]])
                        act("activation", [psb[pb]], [bf("kT")], out=kTv[:, 4 * half:4 * half + 4, tsl],
                            in_=psh(pb)[:, 0:512].rearrange("p (h t) -> p h t", h=4), func=AF.Copy)
                        load(vst, cdv[l, si, tsl, half * 512:(half + 1) * 512], [], [vstb], vtag)
                        pool("tensor_copy", [vstb], [bf("vda")], out=vdav[:, kt, half * 512:(half + 1) * 512], in_=vst)
                    load(stg1[:, 0:256], cckv[l, si, tsl, :], [], [bf("stg1_0")], "stg1_in")
                    load(stg2[:, 0:64], ckr[l, si, tsl, :], [], [bf("stg2_0")], "stg2_in")
                    dve("tensor_copy", [bf("stg1_0")], [bf("ckvt")], out=ckvtv[:, kt, :], in_=stg1[:, 0:256])
                    dve("tensor_copy", [bf("stg2_0")], [bf("krdup")], out=krdup.rearrange("p (a d) -> p a d", a=2),
                        in_=stg2[:, 0:64].unsqueeze(1).broadcast_to([128, 2, 64]))
                    pc = 6 + kt % 2
                    for c in range(2):
                        tp(psh(pc)[:, 512 + c * 128:512 + (c + 1) * 128], ckvtv[:, kt, c * 128:(c + 1) * 128], identb[:],
                           [bf("ckvt"), bf("identb")], [psb[pc]])
                    tp(psh(pc)[:, 768:896], krdup, identb[:], [bf("krdup"), bf("identb")], [psb[pc]])
                    act("activation", [psb[pc]], [bf("ckvT")], out=ckvTv[:, :, tsl],
                        in_=psh(pc)[:, 512:768].rearrange("p (c t) -> p c t", c=2), func=AF.Copy)
                    act("activation", [psb[pc]], [bf("krT")], out=krT[:, tsl], in_=psh(pc)[:, 768:896], func=AF.Copy)

            for j in range(NB):
                if j > 0:
                    fence(B_TMP, FFN_TMP + CD_TMP)
                    fence(R1_ATT, R1_AT + R1_MT)
                N = min(512, Sq - j * 512)
                nTT = N // Pt
                kt_base = kt_base0 + j * 4
                t0 = j * 512
                xbb = [bf("xb%d" % tt) for tt in range(nTT)]
                hTb = [bf("hT%d" % tt) for tt in range(nTT)]
                ktl = [(kt, 128, 0, False) for kt in range(kt_base)]
                if grp == 0:
                    ktl += [(kt_base + i, 128, i * 128, True) for i in range(nTT)]
                else:
                    ktl += [(kt_base, Pt, 0, False)]
                nlast = len(ktl) - 1

                for tt in range(nTT):
                    rows = slice(t0 + tt * Pt, t0 + (tt + 1) * Pt)
                    src = x_in[rows, :] if l == 0 else x1d[rows, :]
                    rd = [] if l == 0 else [bf("x1d%d_%d" % (j, tt))]
                    load(xbv[:Pt, tt, :], src, rd, [xbb[tt]], "xb%d" % tt)
                load(rtok[:Pt, 0:nTT], rt_tok[t0:t0 + N].rearrange("(t p) a d -> p t a d", p=Pt), [], [bf("rtok")], "rtok")

                def norm_hT(gidx):
                    load(gbc[:], bcd[gidx], [], [bf("gbc"), bf("gbc2")], "gbc")
                    sts = []
                    for tt in range(nTT):
                        k = tt % 2
                        sts.append(rms_stats(xbv[:Pt, tt, :], Pt, D, [xbb[tt]], hns[k][:Pt, :], bf("hn%d" % k)))
                    def n_stt(tt):
                        k = tt % 2
                        rs, rsb = sts[tt]
                        dve("scalar_tensor_tensor", [xbb[tt], rsb, bf("gbc"), bf("gbc2")], [bf("hn%d" % k)], out=hns[k][:Pt, :],
                            in0=xbv[:Pt, tt, :], scalar=rs[:Pt, :], in1=gbc[:Pt, :], op0=ALU.mult, op1=ALU.mult)

                    def n_tp(tt):
                        k = tt % 2
                        pb = 6 + k
                        for kc in range(8):
                            tp(psh(pb)[:, kc * 128:kc * 128 + Pt], hns[k][:Pt, kc * 128:(kc + 1) * 128], identb[:Pt, :Pt],
                               [bf("hn%d" % k), bf("identb")], [psb[pb]])

                    def n_ev(tt):
                        pb = 6 + tt % 2
                        dve("tensor_copy", [psb[pb]], [hTb[tt]], out=hTv[:, :, tt * Pt:(tt + 1) * Pt],
                            in_=psh(pb).rearrange("p (k t) -> p k t", k=8)[:, :, :Pt])

                    n_stt(0)
                    for tt in range(nTT):
                        if tt + 1 < nTT:
                            n_stt(tt + 1)
                        n_tp(tt)
                        n_ev(tt)

                ckpt("A")
                norm_hT(l)

                ckpt("B")
                def rope_tok(psin, ncol, tt, outap, outbuf, rbank):
                    k = rot[0] % 2
                    rot[0] += 1
                    s1, s2, b1, b2 = stg1s[k], stg2s[k], bf("stg1_%d" % k), bf("stg2_%d" % k)
                    G = ncol // 64
                    cf = rtok[:Pt, tt, 0, :].unsqueeze(1).broadcast_to([Pt, G, 64])
                    sg = rtok[:Pt, tt, 1, :].unsqueeze(1).broadcast_to([Pt, G, 64])
                    dve("tensor_tensor", [rbank, bf("rtok")], [b1], out=s1[:Pt, :ncol].rearrange("p (g d) -> p g d", d=64),
                        in0=psin.rearrange("p (g d) -> p g d", d=64), in1=cf, op=ALU.mult)
                    dve("tensor_tensor", [rbank, bf("rtok")], [b2], out=s2[:Pt, :ncol].rearrange("p (g d) -> p g d", d=64),
                        in0=psin.rearrange("p (g d) -> p g d", d=64), in1=sg, op=ALU.mult)
                    t1v = s1[:Pt, :ncol].rearrange("p (g a d) -> p g a d", a=2, d=32)
                    t2v = s2[:Pt, :ncol].rearrange("p (g a d) -> p g a d", a=2, d=32)
                    ov = outap.rearrange("p (g a d) -> p g a d", a=2, d=32)
                    pool("tensor_tensor", [b1, b2], [outbuf], out=ov[:, :, 0, :], in0=t1v[:, :, 0, :], in1=t2v[:, :, 1, :], op=ALU.add)
                    pool("tensor_tensor", [b1, b2], [outbuf], out=ov[:, :, 1, :], in0=t1v[:, :, 1, :], in1=t2v[:, :, 0, :], op=ALU.add)

                def postB(g, tt, bi, zp):
                    tks = slice(tt * Pt, (tt + 1) * Pt)
                    kpos = slice(kt_base * 128 + tt * Pt, kt_base * 128 + (tt + 1) * Pt)
                    orow = slice(t0 + tt * Pt, t0 + (tt + 1) * Pt)
                    kti = kt_base + tt
                    ko = rot[1] % 2
                    kq = rot[2] % 2
                    so, sob, sotag = stgos[ko], bf("stgo_%d" % ko), "stgo_%d" % ko
                    qt, qtb = qtoks[kq], bf("qtok_%d" % kq)
                    if g < 2:
                        rot[2] += 1
                        rope_tok(zp, 512, tt, qt[:Pt, :], qtb, psb[bi])
                        for hh in range(4):
                            tp(psh(6)[:, hh * 128:hh * 128 + Pt], qt[:Pt, hh * 128:(hh + 1) * 128], identb[:Pt, :Pt],
                               [qtb, bf("identb")], [psb[6]])
                        act("activation", [psb[6]], [bf("qdaT%d" % g)], out=qdaTv[:, 4 * g:4 * g + 4, tks],
                            in_=psh(6)[:, 0:512].rearrange("p (h t) -> p h t", h=4)[:, :, :Pt], func=AF.Copy)
                    elif g < 4:
                        rot[1] += 1
                        rot[2] += 1
                        gg = g - 2
                        rope_tok(zp, 512, tt, so[:Pt, :], sob, psb[bi])
                        store(o_dk[grp][l, si, orow, gg * 512:(gg + 1) * 512], so[:Pt, :], [sob], sotag, q="act")
                        act("activation", [sob], [qtb], out=qt[:Pt, :], in_=so[:Pt, :], func=AF.Copy)
                        for hh in range(4):
                            tp(psh(6)[:, hh * 128:hh * 128 + Pt], qt[:Pt, hh * 128:(hh + 1) * 128], identb[:Pt, :Pt],
                               [qtb, bf("identb")], [psb[6]])
                        act("activation", [psb[6]], [bf("kT")], out=kTv[:, 4 * gg:4 * gg + 4, kpos],
                            in_=psh(6)[:, 0:512].rearrange("p (h t) -> p h t", h=4)[:, :, :Pt], func=AF.Copy)
                    elif g < 6:
                        rot[1] += 1
                        gg = g - 4
                        act("activation", [psb[bi]], [sob], out=so[:Pt, :], in_=zp, func=AF.Copy)
                        store(o_dv[grp][l, si, orow, gg * 512:(gg + 1) * 512], so[:Pt, :], [sob], sotag, q="act")
                        dve("tensor_copy", [psb[bi]], [bf("vda")], out=vdav[:Pt, kti, gg * 512:(gg + 1) * 512], in_=zp)
                    elif g == 6:
                        rs, rsb = rms_stats(zp, Pt, QL, [psb[bi]], cqn[:Pt, :], bf("cqn"))
                        dve("tensor_scalar", [psb[bi], rsb], [bf("cqn")], out=cqn[:Pt, :], in0=zp, scalar1=rs[:Pt, :], scalar2=None, op0=ALU.mult)
                        for kc in range(3):
                            tp(psh(6)[:, kc * 128:kc * 128 + Pt], cqn[:Pt, kc * 128:(kc + 1) * 128], identb[:Pt, :Pt],
                               [bf("cqn"), bf("identb")], [psb[6]])
                        for kc in range(3):
                            act("activation", [psb[6], bf("spt")], [bf("cqT")], out=cqT[:, kc, tks], in_=psh(6)[:, kc * 128:kc * 128 + Pt],
                                func=AF.Identity, scale=spt[:, spb + SP_GCQ + kc:spb + SP_GCQ + kc + 1])
                    else:
                        rot[1] += 1
                        rs, rsb = rms_stats(zp[:, 0:KVL], Pt, KVL, [psb[bi]], cqn[:Pt, 0:KVL], bf("cqn"))
                        rope_tok(zp[:, KVL:KVL + 64], 64, tt, so[:Pt, KVL:KVL + 64], sob, psb[bi])
                        dve("scalar_tensor_tensor", [psb[bi], rsb, bf("spt")], [sob], out=so[:Pt, 0:KVL], in0=zp[:, 0:KVL],
                            scalar=rs[:Pt, :], in1=spt[:Pt, spb + SP_GCKV:spb + SP_GCKV + KVL], op0=ALU.mult, op1=ALU.mult)
                        store(o_ckv[grp][l, si, orow, :], so[:Pt, 0:KVL], [sob], sotag, q="act")
                        store(o_kr[grp][l, si, orow, :], so[:Pt, KVL:KVL + 64], [sob], sotag + "b", q="act")
                        dve("tensor_copy", [sob], [bf("ckvt")], out=ckvtv[:Pt, kti, :], in_=so[:Pt, 0:KVL])
                        dve("tensor_copy", [sob], [bf("krdup")], out=krdup[:Pt, :].rearrange("p (a d) -> p a d", a=2),
                            in_=so[:Pt, KVL:KVL + 64].unsqueeze(1).broadcast_to([Pt, 2, 64]))
                        for c in range(2):
                            tp(psh(6)[:, c * 128:c * 128 + Pt], ckvtv[:Pt, kti, c * 128:(c + 1) * 128], identb[:Pt, :Pt],
                               [bf("ckvt"), bf("identb")], [psb[6]])
                        tp(psh(6)[:, 256:256 + Pt], krdup[:Pt, :], identb[:Pt, :Pt], [bf("krdup"), bf("identb")], [psb[6]])
                        act("activation", [psb[6]], [bf("ckvT")], out=ckvTv[:, :, kpos],
                            in_=psh(6)[:, 0:256].rearrange("p (c t) -> p c t", c=2)[:, :, :Pt], func=AF.Copy)
                        act("activation", [psb[6]], [bf("krT")], out=krT[:, kpos], in_=psh(6)[:, 256:256 + Pt], func=AF.Copy)

                zb = 0
                pend = []
                for g in (2, 3, 6, 7, 4, 5, 0, 1):
                    ckpt("Bg")
                    wt, wb = wload(l, T_IN + g)
                    wv = wt.rearrange("p (k c) -> p k c", k=8)
                    ncol = IN_NCOLS[g]
                    for tt in range(nTT):
                        bi = zb % 6
                        zb += 1
                        zp = psf(bi)[:Pt, :ncol]
                        tks = slice(tt * Pt, (tt + 1) * Pt)
                        for kc in range(8):
                            mm(zp, hTv[:, kc, tks], wv[:, kc, :ncol], kc == 0, kc == 7, [hTb[tt], wb], [psb[bi]])
                        pend.append((g, tt, bi, zp))
                        if len(pend) > 2:
                            postB(*pend.pop(0))
                while pend:
                    postB(*pend.pop(0))

                ckpt("C")
                fence(CD_TMP, B_TMP)
                NL = lamt[:, 4 * l + 3:4 * l + 4]
                LNC = lamt[:, 4 * L + l:4 * L + l + 1]
                nkt = len(ktl)

                sbank = {}
                sctr = [0]
                pend_epi = []

                def da_qpad(h):
                    qb = bf("qdaT%d" % (h // 4))
                    qpb = bf("qpad%d" % (h % 2))
                    qpv = qpad[h % 2][:].rearrange("p (a n) -> p a n", a=2)
                    dve("tensor_copy", [qb], [qpb], out=qpv[0:64, 0, :N], in_=qdaTv[0:64, h, :N])
                    dve("tensor_copy", [qb], [qpb], out=qpv[64:128, 1, :N], in_=qdaTv[64:128, h, :N])

                def da_S(i):
                    h, ki = divmod(i, nkt)
                    kt, tk, q0, diag = ktl[ki]
                    qpb = bf("qpad%d" % (h % 2))
                    qpv = qpad[h % 2][:].rearrange("p (a n) -> p a n", a=2)
                    nq = N - q0
                    ksl = slice(kt * 128, kt * 128 + tk)
                    sbank[i] = (2 * (i % 2), 2 * (i % 2) + 1)
                    for a in range(2):
                        sb_ = sbank[i][a]
                        mm(psf(sb_)[:tk, :nq], kTv[:, h, ksl], qpv[:, a, q0:N], True, True, [bf("kT"), qpb], [psb[sb_]])
                    if ki == 0 and h + 1 < NH:
                        da_qpad(h + 1)

                def da_epi2(h, sb_):
                    mm(psf(sb_)[:, :N], onesm[:], sqb[:, :N], True, True, [bf("onesm"), bf("sqb")], [psb[sb_]])
                    act("activation", [psb[sb_], bf("epsc")], [bf("nr2")], out=nr2[:, :N], in_=psf(sb_)[:, :N], func=AF.Ln, bias=epsc)
                    act("activation", [bf("nr2"), LB], [bf("nr2")], out=nr2[:, :N], in_=nr2[:, :N], func=AF.Exp, scale=-0.5, bias=LNC)
                    dve("scalar_tensor_tensor", [bf("nr1"), bf("nr2"), bf("spt")], [bf("odaT%d" % h)], out=odaT[:, h, :N], in0=nr1[:, :N],
                        scalar=spt[:, spb + SP_GDA:spb + SP_GDA + 1], in1=nr2[:, :N], op0=ALU.mult, op1=ALU.mult)

                def da_PV(i):
                    h, ki = divmod(i, nkt)
                    kt, tk, q0, diag = ktl[ki]
                    nq = N - q0
                    for a in range(2):
                        sb_ = sbank[i][a]
                        pi = pctr[0] % NPS_
                        pctr[0] += 1
                        P = Psl[pi]
                        Pb = bf("P%d" % pi)
                        act("activation", [psb[sb_]], [Pb], out=P[:tk, :nq], in_=psf(sb_)[:tk, :nq], func=AF.Exp, scale=0.125)
                        if diag:
                            pool("memset", [], [Pb], ap=P[64:128, 0:64], constant=0.0)
                        mm(psf(4 + a)[:, q0:N], vdav[:tk, kt, h * 128:(h + 1) * 128], P[:tk, :nq], ki == 0, ki == nlast,
                           [bf("vda"), Pb], [psb[4 + a]])
                        mm(psf(6 + a)[:, q0:N], onesb[:tk, :], P[:tk, :nq], ki == 0, ki == nlast, [bf("onesb"), Pb], [psb[6 + a]])
                    if ki == nlast:
                        act("activation", [psb[6]], [bf("nr1")], out=nr1[:, :N], in_=psf(6)[:, :N], func=AF.Ln)
                        act("activation", [bf("nr1")], [bf("nr1")], out=nr1[:, :N], in_=nr1[:, :N], func=AF.Exp, scale=-1.0)
                        dve("tensor_tensor", [psb[4], bf("nr1")], [bf("nr1")], out=nr1[:, :N], in0=psf(4)[:, :N], in1=nr1[:, :N], op=ALU.mult)
                        act("activation", [psb[7]], [bf("nr2")], out=nr2[:, :N], in_=psf(7)[:, :N], func=AF.Ln)
                        act("activation", [bf("nr2")], [bf("nr2")], out=nr2[:, :N], in_=nr2[:, :N], func=AF.Exp, scale=-1.0)
                        dve("tensor_tensor", [psb[5], bf("nr2")], [bf("nr2")], out=nr2[:, :N], in0=psf(5)[:, :N], in1=nr2[:, :N], op=ALU.mult)
                        dve("scalar_tensor_tensor", [bf("nr1"), bf("nr2"), LB], [bf("nr1")], out=nr1[:, :N], in0=nr2[:, :N], scalar=NL,
                            in1=nr1[:, :N], op0=ALU.mult, op1=ALU.add)
                        pool("tensor_tensor", [bf("nr1")], [bf("sqb")], out=sqb[:, :N], in0=nr1[:, :N], in1=nr1[:, :N], op=ALU.mult)
                        pend_epi.append([2, h])

                def da_flush(i, force=False):
                    for e_ in list(pend_epi):
                        e_[0] -= 1
                        if e_[0] <= 0 or force:
                            da_epi2(e_[1], sbank[i][0])
                            pend_epi.remove(e_)

                load(rfm[:, :, :N], rt_fm[:, :, t0:t0 + N].rearrange("a p n -> p a n"), [], [bf("rfm")], "rfm")
                wt, wb = wload(l, T_UQR)
                wv = wt[:, 0:3072].rearrange("p (k v c) -> p k v c", k=3, v=2)
                for pr in range(4):
                    for v in range(2):
                        for kc in range(3):
                            mm(psf(6 + v)[:, :N], wv[:, kc, v, pr * 128:(pr + 1) * 128], cqT[:, kc, :N], kc == 0, kc == 2,
                               [wb, bf("cqT")], [psb[6 + v]])
                    dve("tensor_tensor", [psb[6], bf("rfm")], [bf("nr1")], out=nr1[:, :N], in0=psf(6)[:, :N], in1=rfm[:, 0, :N], op=ALU.mult)
                    dve("tensor_tensor", [psb[7], bf("rfm")], [bf("nr2")], out=nr2[:, :N], in0=psf(7)[:, :N], in1=rfm[:, 1, :N], op=ALU.mult)
                    pool("tensor_tensor", [bf("nr1"), bf("nr2")], [bf("qrT")], out=qrTv[0:64, 2 * pr, :N], in0=nr1[0:64, :N], in1=nr2[0:64, :N], op=ALU.add)
                    pool("tensor_tensor", [bf("nr1"), bf("nr2")], [bf("qrT")], out=qrTv[64:128, 2 * pr + 1, :N], in0=nr1[64:128, :N],
                         in1=nr2[64:128, :N], op=ALU.add)
                wtn, wbn = wload(l, T_UQN)
                wvn = wtn[:, 0:3072].rearrange("p (k c) -> p k c", k=3)
                wtk, wbk = wload(l, T_UKT)
                wvk = wtk[:, 0:2048].rearrange("p (h c) -> p h c", h=8)
                wvu = wtk[:, 2048:4096].rearrange("p (h c v) -> p h c v", h=8, c=2)

                qns = [qn, qn_b]
                qlats = [qlat, qlat_b]
                olats = [olat, olat_b]

                def mla_pro(h):
                    k = h % 2
                    for kc in range(3):
                        mm(psf(6)[:, :N], wvn[:, kc, h * 128:(h + 1) * 128], cqT[:, kc, :N], kc == 0, kc == 2, [wbn, bf("cqT")], [psb[6]])
                    dve("tensor_copy", [psb[6]], [bf("qn%d" % k)], out=qns[k][:, :N], in_=psf(6)[:, :N])
                    for c in range(2):
                        mm(psf(7)[:, :N], wvk[:, h, c * 128:(c + 1) * 128], qns[k][:, :N], True, True, [wbk, bf("qn%d" % k)], [psb[7]])
                        dve("tensor_copy", [psb[7]], [bf("qlat%d" % k)], out=qlats[k][:, c, :N], in_=psf(7)[:, :N])

                mla_pro(0)
                nsteps = NH * nkt
                da_qpad(0)
                da_S(0)
                for i in range(nsteps):
                    if i + 1 < nsteps:
                        da_S(i + 1)
                    da_PV(i)
                    da_flush(i)
                da_flush(nsteps - 1, True)

                ckpt("D")
                pend_m = []

                def mla_S(i):
                    h, ki = divmod(i, nkt)
                    kt, tk, q0, diag = ktl[ki]
                    k = h % 2
                    nq = N - q0
                    ksl = slice(kt * 128, kt * 128 + tk)
                    sk = i % 3
                    mm(psf(sk)[:tk, :nq], ckvTv[:, 0, ksl], qlats[k][:, 0, q0:N], True, False, [bf("ckvT"), bf("qlat%d" % k)], [psb[sk]])
                    mm(psf(sk)[:tk, :nq], ckvTv[:, 1, ksl], qlats[k][:, 1, q0:N], False, False, [bf("ckvT"), bf("qlat%d" % k)], [psb[sk]])
                    mm(psf(sk)[:tk, :nq], krT[:, ksl], qrTv[:, h, q0:N], False, True, [bf("krT"), bf("qrT")], [psb[sk]])
                    if ki == 0 and h + 1 < NH:
                        mla_pro(h + 1)

                def mla_epi2(h):
                    k = h % 2
                    for c in range(2):
                        mm(psf(6)[:, :N], wvu[:, h, c, :], olats[k][:, c, :N], c == 0, c == 1, [wbk, bf("olat%d_%d" % (c, k))], [psb[6]])
                    dve("tensor_tensor", [psb[6], bf("nr1")], [bf("omlaT%d" % h)], out=omlaT[:, h, :N], in0=psf(6)[:, :N],
                        in1=nr1[:, :N], op=ALU.mult)

                def mla_PV(i):
                    h, ki = divmod(i, nkt)
                    kt, tk, q0, diag = ktl[ki]
                    k = h % 2
                    nq = N - q0
                    sk = i % 3
                    pi = pctr[0] % NPS_
                    pctr[0] += 1
                    P = Psl[pi]
                    Pb = bf("P%d" % pi)
                    act("activation", [psb[sk]], [Pb], out=P[:tk, :nq], in_=psf(sk)[:tk, :nq], func=AF.Exp, scale=MLA_SCALE)
                    if diag:
                        pool("memset", [], [Pb], ap=P[64:128, 0:64], constant=0.0)
                    for c in range(2):
                        mm(psf(3 + c)[:, q0:N], ckvtv[:tk, kt, c * 128:(c + 1) * 128], P[:tk, :nq], ki == 0, ki == nlast,
                           [bf("ckvt"), Pb], [psb[3 + c]])
                    mm(psf(5)[:, q0:N], onesb[:tk, :], P[:tk, :nq], ki == 0, ki == nlast, [bf("onesb"), Pb], [psb[5]])
                    if ki == nlast:
                        dve("tensor_copy", [psb[3]], [bf("olat0_%d" % k)], out=olats[k][:, 0, :N], in_=psf(3)[:, :N])
                        dve("tensor_copy", [psb[4]], [bf("olat1_%d" % k)], out=olats[k][:, 1, :N], in_=psf(4)[:, :N])
                        act("activation", [psb[5]], [bf("nr1")], out=nr1[:, :N], in_=psf(5)[:, :N], func=AF.Ln)
                        act("activation", [bf("nr1")], [bf("nr1")], out=nr1[:, :N], in_=nr1[:, :N], func=AF.Exp, scale=-1.0)
                        pend_m.append([2, h])

                def mla_flush(force=False):
                    for e_ in list(pend_m):
                        e_[0] -= 1
                        if e_[0] <= 0 or force:
                            mla_epi2(e_[1])
                            pend_m.remove(e_)

                mla_S(0)
                for i in range(nsteps):
                    if i + 1 < nsteps:
                        mla_S(i + 1)
                    mla_PV(i)
                    mla_flush()
                mla_flush(True)

                ckpt("E")
                fence(R1_MT, ["cqT", "qn0", "qlat0", "olat0_0", "olat1_0"])
                for t in range(4):
                    wt, wb = wload(l, T_G + t)
                    wv = wt.rearrange("p (k c) -> p k c", k=8)
                    for i in range(2):
                        jch = 2 * t + i
                        for ab in range(2):
                            bi = (2 * jch + ab) % 4
                            for kc in range(8):
                                mm(psf(bi)[:, :N], wv[:, kc, (2 * ab + i) * 128:(2 * ab + i + 1) * 128], hTv[:, kc, :N], kc == 0, kc == 7,
                                   [wb] + hTb, [psb[bi]])
                            dst = nr1 if ab == 0 else nr2
                            bcol = spb + SP_BG + jch + 8 * ab
                            act("activation", [psb[bi], bf("spt")], [bf("nr1") if ab == 0 else bf("nr2")], out=dst[:, :N], in_=psf(bi)[:, :N],
                                func=AF.Sigmoid, bias=spt[:, bcol:bcol + 1])
                        dve("tensor_tensor", [bf("nr1"), bf("odaT%d" % jch)], [bf("nr1")], out=nr1[:, :N], in0=nr1[:, :N], in1=odaT[:, jch, :N], op=ALU.mult)
                        pool("tensor_tensor", [bf("nr2"), bf("omlaT%d" % jch)], [bf("nr2")], out=nr2[:, :N], in0=nr2[:, :N], in1=omlaT[:, jch, :N], op=ALU.mult)
                        dve("tensor_tensor", [bf("nr1"), bf("nr2")], [bf("mT%d" % jch)], out=mT[:, jch, :N], in0=nr1[:, :N], in1=nr2[:, :N], op=ALU.add)
                mTb = [bf(n) for n in R1_MT]
                for g in range(2):
                    wt, wb = wload(l, T_O + g)
                    wv = wt.rearrange("p (k c) -> p k c", k=8)
                    for tt in range(nTT):
                        bi = 4 + (g * nTT + tt) % 4
                        tks = slice(tt * Pt, (tt + 1) * Pt)
                        for kc in range(8):
                            mm(psf(bi)[:Pt, :], mT[:, kc, tks], wv[:, kc, :], kc == 0, kc == 7, [wb] + mTb, [psb[bi]])
                        dve("tensor_tensor", [psb[bi], xbb[tt]], [xbb[tt]], out=xbv[:Pt, tt, g * 512:(g + 1) * 512], in0=psf(bi)[:Pt, :],
                            in1=xbv[:Pt, tt, g * 512:(g + 1) * 512], op=ALU.add)

                ckpt("F")
                norm_hT(L + l)
                fence(FFN_TMP, CD_TMP + B_TMP)
                fence(R1_AT, R1_ATT + R1_MT)
                wc = spb + SP_WC

                def ffn_mm(it):
                    t, i = divmod(it, 2)
                    if i == 0:
                        wt_, wb_ = wload(l, T_UP + t)
                        fstate[0] = (wt_.rearrange("p (k c) -> p k c", k=8), wb_)
                    wv_, wb_ = fstate[0]
                    for ab in range(2):
                        bi = (2 * it + ab) % 4
                        for kc in range(8):
                            mm(psf(bi)[:, :N], wv_[:, kc, (2 * ab + i) * 128:(2 * ab + i + 1) * 128], hTv[:, kc, :N], kc == 0, kc == 7,
                               [wb_] + hTb, [psb[bi]])

                def ffn_post(it):
                    k = it % 2
                    ca = it
                    for ab in range(2):
                        ch = ca + 22 * ab
                        bi = (2 * it + ab) % 4
                        ue, ueb_ = ues[ab][k], bf("ue%d_%d" % (ab, k))
                        cc, ccb_ = ccs[ab][k], bf("cc%d_%d" % (ab, k))
                        pool("tensor_copy", [bf("carry")], [ueb_], out=ue[:, 0:2], in_=carv[:, :, ch])
                        act("activation", [psb[bi]], [ueb_], out=ue[:, 2:2 + N], in_=psf(bi)[:, :N], func=AF.Copy)
                        act("activation", [psb[bi], bf("spt")], [ccb_], out=cc[:, :N], in_=psf(bi)[:, :N], func=AF.Identity,
                            scale=spt[:, wc + 2 * NCH + ch:wc + 2 * NCH + ch + 1], bias=spt[:, spb + SP_BC + ch:spb + SP_BC + ch + 1])
                        dve("scalar_tensor_tensor", [ueb_, ccb_, bf("spt")], [ccb_], out=cc[:, :N], in0=ue[:, 1:1 + N],
                            scalar=spt[:, wc + NCH + ch:wc + NCH + ch + 1], in1=cc[:, :N], op0=ALU.mult, op1=ALU.add)
                        dve("scalar_tensor_tensor", [ueb_, ccb_, bf("spt")], [ccb_], out=cc[:, :N], in0=ue[:, 0:N],
                            scalar=spt[:, wc + ch:wc + ch + 1], in1=cc[:, :N], op0=ALU.mult, op1=ALU.add)
                        pool("tensor_copy", [ueb_], [bf("carry")], out=carv[:, :, ch], in_=ue[:, N:N + 2])
                    ca_, cb_ = ccs[0][k], ccs[1][k]
                    act("activation", [bf("cc0_%d" % k)], [bf("cc0_%d" % k)], out=ca_[:, :N], in_=ca_[:, :N], func=AF.Silu)
                    dve("tensor_tensor", [bf("cc0_%d" % k), bf("cc1_%d" % k)], [bf("aT%d" % ca)], out=aT[:, ca, :N], in0=ca_[:, :N], in1=cb_[:, :N], op=ALU.mult)

                fstate = [None]
                ffn_mm(0)
                for it in range(22):
                    if it + 1 < 22:
                        ffn_mm(it + 1)
                    ffn_post(it)
                aTb = [bf(n) for n in R1_AT]
                for g in range(2):
                    for ks in range(3):
                        wt, wb = wload(l, T_DN + g * 3 + ks)
                        wv = wt.rearrange("p (k c) -> p k c", k=8)
                        nk = 8 if ks < 2 else 6
                        for tt in range(nTT):
                            bi = 4 + tt
                            tks = slice(tt * Pt, (tt + 1) * Pt)
                            for kk in range(nk):
                                kc = ks * 8 + kk
                                mm(psf(bi)[:Pt, :], aT[:, kc, tks], wv[:, kk, :], kc == 0, kc == 21, [wb] + aTb, [psb[bi]])
                    for tt in range(nTT):
                        bi = 4 + tt
                        dve("tensor_tensor", [psb[bi], xbb[tt]], [xbb[tt]], out=xbv[:Pt, tt, g * 512:(g + 1) * 512], in0=psf(bi)[:Pt, :],
                            in1=xbv[:Pt, tt, g * 512:(g + 1) * 512], op=ALU.add)

                ckpt("G")
                if l < L - 1:
                    for tt in range(nTT):
                        rows = slice(t0 + tt * Pt, t0 + (tt + 1) * Pt)
                        store(x1d[rows, :], xbv[:Pt, tt, :], [xbb[tt]], "xst%d" % tt, w=[bf("x1d%d_%d" % (j, tt))])
                else:
                    load(gbc[:], bcd[2 * L], [], [bf("gbc"), bf("gbc2")], "gbc")
                    for tt in range(nTT):
                        rows = slice(t0 + tt * Pt, t0 + (tt + 1) * Pt)
                        rs, rsb = rms_stats(xbv[:Pt, tt, :], Pt, D, [xbb[tt]], hns[tt % 2][:Pt, :], bf("hn%d" % (tt % 2)))
                        dve("scalar_tensor_tensor", [xbb[tt], rsb, bf("gbc"), bf("gbc2")], [xbb[tt]], out=xbv[:Pt, tt, :], in0=xbv[:Pt, tt, :],
                            scalar=rs[:Pt, :], in1=gbc[:Pt, :], op0=ALU.mult, op1=ALU.mult)
                        store(y_out[rows, :], xbv[:Pt, tt, :], [xbb[tt]], "xst%d" % tt)
                if j == NB - 1:
                    tp(psf(0)[:88, :128], carry[:, :], identf[:, :], [bf("carry"), bf("identf")], [psb[0]])
                    dve("tensor_copy", [psb[0]], [bf("stgc")], out=stgc[:88, :], in_=psf(0)[:88, :128])
                    store(o_cv[grp][l, si].rearrange("t (c p) -> (t c) p", p=128), stgc[:88, :], [bf("stgc")], "stgc_out")

    try:
        for si in range(NPS):
            run_sequence(0, si, S, 128, 0)
        for si in range(NSS):
            run_sequence(1, si, SD, SD, PAST)
    except _Stop:
        pass

    for e in R.ENGS:
        for o in R.ops[e]:
            for d in o.deps:
                if not d.dma:
                    d.sig = True
    cnt = {e: 0 for e in R.ENGS}
    for e in R.ENGS:
        for o in R.ops[e]:
            if (not o.dma) and o.sig:
                cnt[e] += 1
                o.val = cnt[e]
    tag_total = {t: 16 * len(v) for t, v in R.tag_ops.items()}
    sems = {}
    for e in ("pe", "act", "dve", "pool"):
        sems[e] = nc.alloc_semaphore("s_" + e)
    for t in R.tag_ops:
        sems["t_" + t] = nc.alloc_semaphore("t_" + t)

    def sigof(d):
        if d.dma:
            v = tag_total[d.tag] if d.tag in R.wait_all_tags else d.val
            return "t_" + d.tag, v
        return d.eng, d.val

    engh = {"pe": "tensor", "act": "scalar", "dve": "vector", "pool": "gpsimd", "sp": "sync"}

    def emit(engname, eng):
        known = {}
        for o in R.ops[engname]:
            need = {}
            for d in o.deps:
                s, v = sigof(d)
                if known.get(s, 0) < v and need.get(s, 0) < v:
                    need[s] = v
            for s, v in need.items():
                eng.wait_ge(sems[s], v)
                known[s] = v
            ins = o.fn(eng)
            if o.dma:
                ins.then_inc(sems["t_" + o.tag], 16)
            elif o.sig:
                ins.then_inc(sems[engname], 1)
        if engname == "sp":
            fin = {}
            for d in R.final_deps:
                s, v = sigof(d)
                fin[s] = max(fin.get(s, 0), v)
            for s, v in fin.items():
                eng.wait_ge(sems[s], v)

    with nc.Block() as block:
        @block.tensor
        def _(e):
            emit("pe", e)

        @block.scalar
        def _(e):
            emit("act", e)

        @block.vector
        def _(e):
            emit("dve", e)

        @block.gpsimd
        def _(e):
            emit("pool", e)

        @block.sync
        def _(e):
            emit("sp", e)
    es.close()
    nops = {e: len(R.ops[e]) for e in R.ENGS}
    return nc, nops


def _pack_weights(w_in, w_uq, w_uk, w_uv, w_o, w_up, w_down, L):
    wp = np.zeros((L, NT, 128, 4096), np.float32)
    for l in range(L):
        wi = w_in[l]
        colsets = [np.arange(g * 512, (g + 1) * 512) for g in range(6)]
        colsets.append(np.arange(3072, 3456))
        colsets.append(np.arange(3456, 3776))
        for g, cs in enumerate(colsets):
            t = wi[:, cs].reshape(8, 128, len(cs)).transpose(1, 0, 2)
            buf = np.zeros((128, 8, 512), np.float32)
            buf[:, :, :len(cs)] = t
            wp[l, T_IN + g] = buf.reshape(128, 4096)
        uq = w_uq[l]
        t = uq[:, :, :128].reshape(3, 128, 8 * 128).transpose(1, 0, 2)
        wp[l, T_UQN, :, :3072] = t.reshape(128, 3072)
        rp = uq[:, :, 128:]
        swap = np.concatenate([np.arange(32, 64), np.arange(0, 32)])
        r0 = rp.reshape(3, 128, 512)
        r1 = rp[:, :, swap].reshape(3, 128, 512)
        t = np.stack([r0, r1], axis=2).transpose(1, 0, 2, 3)
        wp[l, T_UQR, :, :3072] = t.reshape(128, 3072)
        wp[l, T_UKT, :, :2048] = w_uk[l].transpose(2, 0, 1).reshape(128, 2048)
        t = w_uv[l].reshape(8, 2, 128, 128).transpose(2, 0, 1, 3)
        wp[l, T_UKT, :, 2048:4096] = t.reshape(128, 2048)
        for tt in range(4):
            cs = np.concatenate([3776 + np.arange((2 * tt) * 128, (2 * tt + 2) * 128),
                                 3776 + 1024 + np.arange((2 * tt) * 128, (2 * tt + 2) * 128)])
            wp[l, T_G + tt] = wi[:, cs].reshape(8, 128, 512).transpose(1, 0, 2).reshape(128, 4096)
        for g in range(2):
            wp[l, T_O + g] = w_o[l][:, g * 512:(g + 1) * 512].reshape(8, 128, 512).transpose(1, 0, 2).reshape(128, 4096)
        for tt in range(11):
            cs = np.concatenate([np.arange(2 * tt * 128, (2 * tt + 2) * 128), DFF + np.arange(2 * tt * 128, (2 * tt + 2) * 128)])
            wp[l, T_UP + tt] = w_up[l][:, cs].reshape(8, 128, 512).transpose(1, 0, 2).reshape(128, 4096)
        for g in range(2):
            for ks in range(3):
                nk = 8 if ks < 2 else 6
                rows = slice(ks * 1024, ks * 1024 + nk * 128)
                t = w_down[l][rows, g * 512:(g + 1) * 512].reshape(nk, 128, 512).transpose(1, 0, 2)
                buf = np.zeros((128, 8, 512), np.float32)
                buf[:, :nk] = t
                wp[l, T_DN + g * 3 + ks] = buf.reshape(128, 4096)
    return wp


def _pack_small(b_gate, g_cq, g_da_head, w_conv, b_conv, lam_q1, lam_k1, lam_q2, lam_k2, g_ckv, L):
    sp = np.zeros((128, L * SPW), np.float32)
    for l in range(L):
        b = l * SPW
        sp[:, b + SP_BG:b + SP_BG + 16] = b_gate[l].reshape(16, 128).T
        sp[:, b + SP_GCQ:b + SP_GCQ + 3] = g_cq[l].reshape(3, 128).T
        sp[:, b + SP_GDA] = g_da_head[l]
        for j in range(3):
            sp[:, b + SP_WC + j * NCH:b + SP_WC + (j + 1) * NCH] = w_conv[l, j].reshape(NCH, 128).T
        sp[:, b + SP_BC:b + SP_BC + NCH] = b_conv[l].reshape(NCH, 128).T
        for j, v in enumerate((lam_q1, lam_k1, lam_q2, lam_k2)):
            sp[:, b + SP_LAM + 64 * j:b + SP_LAM + 64 * (j + 1)] = v[l][None, :]
        sp[:, b + SP_GCKV:b + SP_GCKV + KVL] = g_ckv[l][None, :]
    return sp


def _rope_tables(pos):
    half = 32
    inv = np.power(np.float32(10000.0), -np.arange(half, dtype=np.float32) / np.float32(half)).astype(np.float32)
    ang = (pos.astype(np.float32)[:, None] * inv[None, :]).astype(np.float32)
    c = np.cos(ang).astype(np.float32)
    s = np.sin(ang).astype(np.float32)
    n = len(pos)
    tok = np.zeros((n, 2, 64), np.float32)
    tok[:, 0, :32] = c
    tok[:, 0, 32:] = c
    tok[:, 1, :32] = s
    tok[:, 1, 32:] = -s
    fm = np.zeros((2, 128, n), np.float32)
    for p in range(128):
        fm[0, p] = c[:, p % 32]
        fm[1, p] = (-s[:, p % 32]) if (p % 64) < 32 else s[:, p % 32]
    return tok, fm


_CACHE = {}


def run(cfg, inputs, n_cores):
    L, S, NPS, NSS, SD, PAST = cfg.L, cfg.S, cfg.NPS, cfg.NSS, cfg.SD, cfg.PAST
    f = lambda a: np.ascontiguousarray(np.asarray(a, dtype=np.float32))
    I = {k: f(v) for k, v in inputs.items()}
    key = (L, S, NPS, NSS, SD, PAST)
    if key not in _CACHE:
        _CACHE[key] = build(cfg)
    nc, nops = _CACHE[key]
    wp = _pack_weights(I["w_in"], I["w_uq"], I["w_uk"], I["w_uv"], I["w_o"], I["w_up"], I["w_down"], L)
    sp = _pack_small(I["b_gate"], I["g_cq"], I["g_da_head"], I["w_conv"], I["b_conv"], I["lam_q1"], I["lam_k1"],
                     I["lam_q2"], I["lam_k2"], I["g_ckv"], L)
    bcd = np.zeros((2 * L + 1, 128, D), np.float32)
    for l in range(L):
        bcd[l] = I["g_attn"][l][None, :]
        bcd[L + l] = I["g_ffn"][l][None, :]
    bcd[2 * L] = I["g_final"][None, :]
    rtp, rfp = _rope_tables(np.arange(S))
    rts, rfs = _rope_tables(PAST + np.arange(SD))
    ident = np.eye(128, dtype=np.float32)
    in_maps = []
    for c in range(n_cores):
        ps_ = slice(c * NPS, (c + 1) * NPS)
        ss_ = slice(c * NSS, (c + 1) * NSS)
        in_maps.append({
            "xp": I["x_prompt"][ps_], "xs": I["x_sample"][ss_],
            "cdk": np.ascontiguousarray(I["cache_dk"][:, ss_].reshape(L, NSS, PAST, 1024)),
            "cdv": np.ascontiguousarray(I["cache_dv"][:, ss_].reshape(L, NSS, PAST, 1024)),
            "cckv": np.ascontiguousarray(I["cache_ckv"][:, ss_]),
            "ckr": np.ascontiguousarray(I["cache_krope"][:, ss_]),
            "sconv": np.ascontiguousarray(I["state_conv"][:, ss_]),
            "wpack": wp, "spd": sp, "bcd": bcd, "rtp": rtp, "rts": rts, "rfp": rfp, "rfs": rfs, "identd": ident,
        })
    res = run_bass_kernel_spmd(nc, in_maps, core_ids=list(range(n_cores)))
    rs = res.results
    cat0 = lambda k: np.concatenate([r[k] for r in rs], axis=0)
    cat1 = lambda k: np.concatenate([r[k] for r in rs], axis=1)
    Bp, Bs = NPS * n_cores, NSS * n_cores
    return (cat0("o_yp"), cat0("o_ys"),
            cat1("o_dkp").reshape(L, Bp, S, NH, 128), cat1("o_dvp").reshape(L, Bp, S, NH, 128),
            cat1("o_ckvp"), cat1("o_krp"), cat1("o_cvp"),
            cat1("o_dks").reshape(L, Bs, SD, NH, 128), cat1("o_dvs").reshape(L, Bs, SD, NH, 128),
            cat1("o_ckvs"), cat1("o_krs"), cat1("o_cvs"))


def kernel(**inputs):
    cfg = Cfg(L=2, S=2048, NPS=4, NSS=2, SD=32, PAST=2048)
    return run(cfg, inputs, 8)
```
